# Optimizing a Trainium2 kernel written in Bass

```python
import jax, jax.numpy as jnp
from jax import lax
import numpy as np

D_MODEL = 1024
BATCH = 8
SEQ = 8192
DEPTH = 1
DEC_BATCH = 16
DEC_SEQ = 32
PAST_LEN = 2048

CHUNK = 64
N_Q_HEADS = 16
N_KV_HEADS = 4
HEAD_DIM = 64
Q_PER_KV = N_Q_HEADS // N_KV_HEADS
ATT_WIDTH = N_Q_HEADS * HEAD_DIM
KV_WIDTH = N_KV_HEADS * HEAD_DIM
WINDOW = 128
WINDOW_CHUNKS = WINDOW // CHUNK
ROT_DIM = HEAD_DIM // 4
ROPE_THETA = 500000.0
SSM_WIDTH = 2 * D_MODEL
SSM_HEAD_DIM = 64
SSM_HEADS = SSM_WIDTH // SSM_HEAD_DIM
SSM_GROUPS = 4
HEADS_PER_GROUP = SSM_HEADS // SSM_GROUPS
SSM_STATE = 128
BC_WIDTH = SSM_GROUPS * SSM_STATE
CONV_W = 4
CONV_CH = SSM_WIDTH + 2 * BC_WIDTH
EPS = 1e-6
COL_SIZES = (ATT_WIDTH, KV_WIDTH, KV_WIDTH, ATT_WIDTH, SSM_WIDTH, CONV_CH, SSM_HEADS, D_MODEL, D_MODEL)
IN_WIDTH = ATT_WIDTH + 2 * KV_WIDTH + ATT_WIDTH + SSM_WIDTH + CONV_CH + SSM_HEADS + 2 * D_MODEL

kernel_name = 'hybrid_swa_sink_ssd_streaming_step'


def _rms_norm(x, w):
    xf = x.astype(jnp.float32)
    y = xf * lax.rsqrt(jnp.mean(xf * xf, axis=-1, keepdims=True) + EPS)
    return (y * w.astype(jnp.float32)).astype(x.dtype)


def _split_cols(proj):
    parts, off = [], 0
    for size in COL_SIZES:
        parts.append(proj[..., off:off + size])
        off += size
    return parts


def _partial_rope(x, pos):
    half = ROT_DIM // 2
    inv = ROPE_THETA ** (-(jnp.arange(half, dtype=jnp.float32) * 2.0 / ROT_DIM))
    ang = pos.astype(jnp.float32)[:, None] * inv[None, :]
    cos = jnp.cos(ang)[None, :, None, :]
    sin = jnp.sin(ang)[None, :, None, :]
    xf = x.astype(jnp.float32)
    x1, x2, rest = xf[..., :half], xf[..., half:ROT_DIM], xf[..., ROT_DIM:]
    return jnp.concatenate([x1 * cos - x2 * sin, x2 * cos + x1 * sin, rest], axis=-1).astype(x.dtype)


def _sink_softmax(s, sinks):
    sk = sinks.astype(jnp.float32).reshape(N_KV_HEADS, Q_PER_KV, 1, 1)
    m = jnp.maximum(jnp.max(s, axis=-1, keepdims=True), sk)
    p = jnp.exp(s - m)
    return p / (jnp.sum(p, axis=-1, keepdims=True) + jnp.exp(sk - m))


def _banded_window_attention(q, k, v, sinks):
    bsz, L = q.shape[:2]
    n_c = L // CHUNK
    qc = q.reshape(bsz, n_c, CHUNK, N_KV_HEADS, Q_PER_KV, HEAD_DIM)
    pad = ((0, 0), (WINDOW_CHUNKS, 0), (0, 0), (0, 0), (0, 0))
    kp = jnp.pad(k.reshape(bsz, n_c, CHUNK, N_KV_HEADS, HEAD_DIM), pad)
    vp = jnp.pad(v.reshape(bsz, n_c, CHUNK, N_KV_HEADS, HEAD_DIM), pad)
    k_band = jnp.concatenate([kp[:, j:j + n_c] for j in range(WINDOW_CHUNKS + 1)], axis=2)
    v_band = jnp.concatenate([vp[:, j:j + n_c] for j in range(WINDOW_CHUNKS + 1)], axis=2)
    key_chunk = (jnp.arange(n_c)[:, None] - WINDOW_CHUNKS
                 + (jnp.arange((WINDOW_CHUNKS + 1) * CHUNK) // CHUNK)[None, :])
    valid = key_chunk >= 0
    s = jnp.einsum('bcqhgd,bckhd->bchgqk', qc, k_band).astype(jnp.float32) * (HEAD_DIM ** -0.5)
    s = jnp.where(valid[None, :, None, None, None, :], s, -jnp.inf)
    p = _sink_softmax(s, sinks).astype(v.dtype)
    out = jnp.einsum('bchgqk,bckhd->bcqhgd', p, v_band)
    return out.reshape(bsz, L, ATT_WIDTH)


def _cached_window_attention(q, k, v, cache_k, cache_v, sinks):
    bsz, L = q.shape[:2]
    k_all = jnp.concatenate([cache_k.astype(k.dtype), k], axis=1)
    v_all = jnp.concatenate([cache_v.astype(v.dtype), v], axis=1)
    qg = q.reshape(bsz, L, N_KV_HEADS, Q_PER_KV, HEAD_DIM)
    s = jnp.einsum('bqhgd,bkhd->bhgqk', qg, k_all).astype(jnp.float32) * (HEAD_DIM ** -0.5)
    p = _sink_softmax(s, sinks).astype(v.dtype)
    out = jnp.einsum('bhgqk,bkhd->bqhgd', p, v_all)
    return out.reshape(bsz, L, ATT_WIDTH)


def _causal_dwconv(x, buf, w, b):
    L = x.shape[1]
    xpad = jnp.concatenate([buf.astype(x.dtype), x], axis=1)
    y = b[None, None, :]
    for tap in range(CONV_W):
        y = y + xpad[:, tap:tap + L] * w[tap][None, None, :]
    return y, xpad[:, L:]


def _ssd_chunk(h0, xs, dt, bm, cm, a):
    bsz, Q = xs.shape[:2]
    x5 = xs.reshape(bsz, Q, SSM_GROUPS, HEADS_PER_GROUP, SSM_HEAD_DIM).astype(jnp.float32)
    dt4 = dt.reshape(bsz, Q, SSM_GROUPS, HEADS_PER_GROUP)
    cum = jnp.cumsum(dt4 * a.reshape(SSM_GROUPS, HEADS_PER_GROUP), axis=1)
    seg = cum[:, :, None] - cum[:, None, :]
    causal = jnp.tril(jnp.ones((Q, Q), dtype=bool))[None, :, :, None, None]
    decay = jnp.exp(jnp.where(causal, seg, -jnp.inf))
    bf, cf = bm.astype(jnp.float32), cm.astype(jnp.float32)
    cb = jnp.einsum('bign,bjgn->bijg', cf, bf)
    y = jnp.einsum('bijg,bijgh,bjgh,bjghp->bighp', cb, decay, dt4, x5)
    h0g = h0.reshape(bsz, SSM_GROUPS, HEADS_PER_GROUP, SSM_HEAD_DIM, SSM_STATE)
    y = y + jnp.einsum('bign,bghpn,bigh->bighp', cf, h0g, jnp.exp(cum))
    to_end = jnp.exp(cum[:, -1:] - cum) * dt4
    h_new = (jnp.exp(cum[:, -1])[..., None, None] * h0g
             + jnp.einsum('bjgn,bjgh,bjghp->bghpn', bf, to_end, x5))
    return (y.reshape(bsz, Q, SSM_HEADS, SSM_HEAD_DIM),
            h_new.reshape(bsz, SSM_HEADS, SSM_HEAD_DIM, SSM_STATE))


def _layer(x, pos, is_prompt, cache_k, cache_v, conv_buf, ssm_h,
           pre_norm_w, w_in, conv_w, conv_b, dt_bias, a_log, d_skip, sinks,
           ssm_norm_w, w_attn_o, w_ssm_o, w_out, post_norm_w):
    bsz, L, _ = x.shape
    h = _rms_norm(x, pre_norm_w)
    proj = h @ w_in
    q, k, v, g_att, z, xbc, dt_raw, gate_a, gate_s = _split_cols(proj)

    q = _partial_rope(q.reshape(bsz, L, N_Q_HEADS, HEAD_DIM), pos)
    k = _partial_rope(k.reshape(bsz, L, N_KV_HEADS, HEAD_DIM), pos)
    v = v.reshape(bsz, L, N_KV_HEADS, HEAD_DIM)
    if is_prompt:
        att = _banded_window_attention(q, k, v, sinks)
        n_keep = min(WINDOW, L)
        new_k, new_v = k[:, L - n_keep:], v[:, L - n_keep:]
    else:
        att = _cached_window_attention(q, k, v, cache_k, cache_v, sinks)
        new_k, new_v = k, v
    branch_a = (att * jax.nn.silu(g_att)) @ w_attn_o

    xbc_c, new_buf = _causal_dwconv(xbc, conv_buf, conv_w, conv_b)
    xbc_c = jax.nn.silu(xbc_c)
    xs = xbc_c[..., :SSM_WIDTH].reshape(bsz, L, SSM_HEADS, SSM_HEAD_DIM)
    bm = xbc_c[..., SSM_WIDTH:SSM_WIDTH + BC_WIDTH].reshape(bsz, L, SSM_GROUPS, SSM_STATE)
    cm = xbc_c[..., SSM_WIDTH + BC_WIDTH:].reshape(bsz, L, SSM_GROUPS, SSM_STATE)
    dt = jax.nn.softplus(dt_raw.astype(jnp.float32) + dt_bias.astype(jnp.float32))
    a = -jnp.exp(a_log.astype(jnp.float32))
    h0 = ssm_h.astype(jnp.float32)
    if is_prompt:
        n_c = L // CHUNK
        to_chunks = lambda t: jnp.swapaxes(t.reshape((bsz, n_c, CHUNK) + t.shape[2:]), 0, 1)

        def step(carry, inp):
            y_c, carry = _ssd_chunk(carry, inp[0], inp[1], inp[2], inp[3], a)
            return carry, y_c

        h_new, ys = lax.scan(step, h0, (to_chunks(xs), to_chunks(dt), to_chunks(bm), to_chunks(cm)))
        y = jnp.swapaxes(ys, 0, 1).reshape(bsz, L, SSM_HEADS, SSM_HEAD_DIM)
    else:
        y, h_new = _ssd_chunk(h0, xs, dt, bm, cm, a)
    y = y + xs.astype(jnp.float32) * d_skip.astype(jnp.float32)[None, None, :, None]
    yz = y.reshape(bsz, L, SSM_WIDTH).astype(x.dtype) * jax.nn.silu(z)
    yg = yz.reshape(bsz, L, SSM_GROUPS, SSM_WIDTH // SSM_GROUPS).astype(jnp.float32)
    yg = yg * lax.rsqrt(jnp.mean(yg * yg, axis=-1, keepdims=True) + EPS)
    yz = (yg.reshape(bsz, L, SSM_WIDTH) * ssm_norm_w.astype(jnp.float32)).astype(x.dtype)
    branch_s = yz @ w_ssm_o

    merged = jax.nn.sigmoid(gate_a) * branch_a + jax.nn.sigmoid(gate_s) * branch_s
    out = merged @ w_out
    return x + _rms_norm(out, post_norm_w), new_k, new_v, new_buf, h_new


def setup_inputs(seed: int = 0) -> dict:
    key = jax.random.key(seed)
    ks = jax.random.split(key, 24)
    f32 = jnp.float32
    n = lambda k_, shape, s: jax.random.normal(k_, shape, f32) * s
    win_cache = min(WINDOW, PAST_LEN)
    dt0 = jnp.exp(jax.random.uniform(ks[10], (DEPTH, SSM_HEADS), f32, np.log(1e-3), np.log(1e-1)))
    return {
        'x_prompt': n(ks[0], (BATCH, SEQ, D_MODEL), 1.0),
        'x_sample': n(ks[1], (DEC_BATCH, DEC_SEQ, D_MODEL), 1.0),
        'cache_k': n(ks[2], (DEPTH, DEC_BATCH, win_cache, N_KV_HEADS, HEAD_DIM), 1.0),
        'cache_v': n(ks[3], (DEPTH, DEC_BATCH, win_cache, N_KV_HEADS, HEAD_DIM), 1.0),
        'state_conv': n(ks[4], (DEPTH, DEC_BATCH, CONV_W - 1, CONV_CH), 1.0),
        'state_ssm': n(ks[5], (DEPTH, DEC_BATCH, SSM_HEADS, SSM_HEAD_DIM, SSM_STATE), 0.5),
        'pre_norm_w': 1.0 + n(ks[6], (DEPTH, D_MODEL), 0.01),
        'w_in': n(ks[7], (DEPTH, D_MODEL, IN_WIDTH), D_MODEL ** -0.5),
        'conv_w': n(ks[8], (DEPTH, CONV_W, CONV_CH), CONV_W ** -0.5),
        'conv_b': n(ks[9], (DEPTH, CONV_CH), 0.01),
        'dt_bias': dt0 + jnp.log(-jnp.expm1(-dt0)),
        'a_log': jnp.log(jax.random.uniform(ks[11], (DEPTH, SSM_HEADS), f32, 1.0, 16.0)),
        'd_skip': 1.0 + n(ks[12], (DEPTH, SSM_HEADS), 0.1),
        'sinks': n(ks[13], (DEPTH, N_Q_HEADS), 0.5),
        'ssm_norm_w': 1.0 + n(ks[14], (DEPTH, SSM_WIDTH), 0.01),
        'w_attn_o': n(ks[15], (DEPTH, ATT_WIDTH, D_MODEL), ATT_WIDTH ** -0.5),
        'w_ssm_o': n(ks[16], (DEPTH, SSM_WIDTH, D_MODEL), SSM_WIDTH ** -0.5),
        'w_out': n(ks[17], (DEPTH, D_MODEL, D_MODEL), D_MODEL ** -0.5),
        'post_norm_w': 1.0 + n(ks[18], (DEPTH, D_MODEL), 0.01),
    }


def reference(x_prompt, x_sample, cache_k, cache_v, state_conv, state_ssm,
              pre_norm_w, w_in, conv_w, conv_b, dt_bias, a_log, d_skip, sinks,
              ssm_norm_w, w_attn_o, w_ssm_o, w_out, post_norm_w):
    bp, lp = x_prompt.shape[:2]
    ls = x_sample.shape[1]
    pos_p = jnp.arange(lp, dtype=jnp.float32)
    pos_s = PAST_LEN + jnp.arange(ls, dtype=jnp.float32)
    zero_buf = jnp.zeros((bp, CONV_W - 1, CONV_CH), x_prompt.dtype)
    zero_h = jnp.zeros((bp, SSM_HEADS, SSM_HEAD_DIM, SSM_STATE), jnp.float32)
    yp, ys = x_prompt, x_sample
    kp_l, vp_l, cp_l, sp_l, ks_l, vs_l, cs_l, ss_l = [], [], [], [], [], [], [], []
    for layer in range(DEPTH):
        wts = (pre_norm_w[layer], w_in[layer], conv_w[layer], conv_b[layer], dt_bias[layer],
               a_log[layer], d_skip[layer], sinks[layer], ssm_norm_w[layer], w_attn_o[layer],
               w_ssm_o[layer], w_out[layer], post_norm_w[layer])
        yp, kp, vp, cp, sp = _layer(yp, pos_p, True, None, None, zero_buf, zero_h, *wts)
        ys, ks_, vs, cs, ss = _layer(ys, pos_s, False, cache_k[layer], cache_v[layer],
                                     state_conv[layer], state_ssm[layer], *wts)
        kp_l.append(kp); vp_l.append(vp); cp_l.append(cp); sp_l.append(sp)
        ks_l.append(ks_); vs_l.append(vs); cs_l.append(cs); ss_l.append(ss)
    return (yp, ys,
            jnp.stack(kp_l), jnp.stack(vp_l), jnp.stack(cp_l), jnp.stack(sp_l),
            jnp.stack(ks_l), jnp.stack(vs_l), jnp.stack(cs_l), jnp.stack(ss_l))
```

```python
import numpy as np
import concourse.bass as bass
import concourse.mybir as mybir
from concourse.bass_utils import run_bass_kernel_spmd

F32 = mybir.dt.float32
BF16 = mybir.dt.bfloat16
AF = mybir.ActivationFunctionType
ALU = mybir.AluOpType

D = 1024
INW = 9760
NEGV = -30000.0
import os as _os0
_os_env = _os0.environ.get
SAME_ENG_SYNC = bool(int(_os_env("K_SES", "1")))
import os as _os
STOP = float(_os.environ.get('K_STOP', '99'))


class _Stop(Exception):
    pass


def stage(k):
    if STOP <= k:
        raise _Stop()


def C(name, *a, **k):
    return (name, a, k)


class Buf:
    __slots__ = ("w", "r", "name")

    def __init__(self, name):
        self.name = name
        self.w = None
        self.r = {}


class Tracker:
    ENG = ("pe", "act", "dve", "pool", "sp")

    def __init__(self, nc, sems, dma_sems):
        self.nc = nc
        self.prog = {e: [] for e in self.ENG}
        self.cnt = {e: 0 for e in self.ENG}
        self.sem = sems
        self.semid = {e: ("E", e) for e in self.ENG}
        self.waited = {e: {} for e in self.ENG}
        self.dma_sems = dma_sems
        self.dma_cnt = [0] * len(dma_sems)
        self.dma_rr = 0
        self.dma_rr_sw = len(dma_sems) // 2
        self.bufs = {}
        self.out_events = []
        self.n = 0
        self.phase_of = {}
        self._rec = None
        self.phases = {}

    def set_phase(self, names, ph):
        self.phases.setdefault(ph, [])
        for n_ in names:
            self.phase_of[n_] = ph
            self.phases[ph].append(n_)

    def B(self, name):
        b = self.bufs.get(name)
        if b is None:
            b = self.bufs[name] = Buf(name)
        return b

    def _wait(self, eng, ev):
        key, val, sem, src = ev
        if src == eng and eng in ("pe",):
            return
        if src == eng and not SAME_ENG_SYNC and eng in ("act", "dve", "pool"):
            return
        if self.waited[eng].get(key, 0) >= val:
            return
        self.waited[eng][key] = val
        self.prog[eng].append(lambda e, s=sem, v=val: e.wait_ge(s, v))

    def _deps(self, eng, R, W):
        evs = []
        for b in R:
            b = self.B(b) if isinstance(b, str) else b
            if b.w is not None:
                evs.append(b.w)
            if b.name.startswith("ps"):
                evs.extend(ev for ev in b.r.values() if ev[3] != eng)
        for b in W:
            b = self.B(b) if isinstance(b, str) else b
            if b.w is not None:
                evs.append(b.w)
            evs.extend(b.r.values())
            ph = self.phase_of.get(b.name)
            if ph is not None:
                for oph, names in self.phases.items():
                    if oph == ph or oph[:2] != ph[:2]:
                        continue
                    for n_ in names:
                        ob = self.bufs.get(n_)
                        if ob is None:
                            continue
                        if ob.w is not None:
                            evs.append(ob.w)
                        evs.extend(ob.r.values())
                        ob.w = None
                        ob.r = {}
        for ev in evs:
            self._wait(eng, ev)

    def _mark(self, ev, R, W):
        for b in R:
            b = self.B(b) if isinstance(b, str) else b
            old = b.r.get(ev[0])
            if old is None or old[1] < ev[1]:
                b.r[ev[0]] = ev
        for b in W:
            b = self.B(b) if isinstance(b, str) else b
            b.w = ev
            b.r = {}

    def rec(self, fn, *a):
        saved = self._rec
        self._rec = lst = []
        try:
            fn(*a)
        finally:
            self._rec = saved
        return lst

    def play(self, lists):
        lists = [l for l in lists if l]
        pos = [0] * len(lists)
        while True:
            best, bi = None, -1
            for i, l in enumerate(lists):
                if pos[i] < len(l):
                    fr = pos[i] / len(l)
                    if best is None or fr < best:
                        best, bi = fr, i
            if bi < 0:
                break
            item = lists[bi][pos[bi]]
            pos[bi] += 1
            if item[0] == "op":
                self.op(*item[1:])
            else:
                self.dma(item[1], item[2], item[3], R=item[4], W=item[5], is_out=item[6], **item[7])

    def op(self, eng, fn, R=(), W=()):
        if self._rec is not None:
            self._rec.append(("op", eng, fn, tuple(R), tuple(W)))
            return None
        self.n += 1
        self._deps(eng, R, W)
        self.cnt[eng] += 1
        sem = self.sem[eng]
        ev = (self.semid[eng], self.cnt[eng], sem, eng)
        nm_, a_, k_ = fn
        self.prog[eng].append(lambda e, nm_=nm_, a_=a_, k_=k_, s=sem: getattr(e, nm_)(*a_, **k_).then_inc(s, 1))
        self._mark(ev, R, W)
        return ev

    def dma(self, q, out, in_, R=(), W=(), is_out=False, **kw):
        if self._rec is not None:
            self._rec.append(("dma", q, out, in_, tuple(R), tuple(W), is_out, kw))
            return None
        self.n += 1
        self._deps(q, R, W)
        half = len(self.dma_sems) // 2
        if q == "pool":
            i = self.dma_rr_sw
            self.dma_rr_sw = half + (self.dma_rr_sw + 1 - half) % (len(self.dma_sems) - half)
        else:
            i = self.dma_rr
            self.dma_rr = (self.dma_rr + 1) % half
        sem = self.dma_sems[i]
        prev = self.dma_cnt[i]
        if prev > 0:
            self._wait(q, (("D", i), prev, sem, None))
        self.dma_cnt[i] = prev + 16
        ev = (("D", i), prev + 16, sem, None)
        self.prog[q].append(lambda e, o=out, a=in_, s=sem, k=kw: e.dma_start(out=o, in_=a, **k).then_inc(s, 16))
        self._mark(ev, R, W)
        if is_out:
            self.out_events.append(ev)
        return ev

    def finish(self):
        for ev in self.out_events:
            self._wait("sp", ev)
        for i, c in enumerate(self.dma_cnt):
            if c > 0:
                self._wait("sp", (("D", i), c, self.dma_sems[i], None))
        for e in ("pe", "act", "dve", "pool"):
            if self.cnt[e] > 0:
                self._wait("sp", (self.semid[e], self.cnt[e], self.sem[e], e))


class TT:
    def __init__(self, h):
        self.h = h
        a = h[:] if not isinstance(h, bass.AP) else h
        self.t = a.tensor
        self.ps = a.ap[0][0]
        self.base = a.offset

    def v(self, p0, p1, off, pat):
        return bass.AP(self.t, self.base + p0 * self.ps + off, [[self.ps, p1 - p0]] + [list(x) for x in pat])


def build_program(NTP, with_sample=True):
    NG = NTP // 4
    NTAB = NTP + 1
    nc = bass.Bass("TRN2", target_bir_lowering=False)

    def din(name, shape):
        return nc.dram_tensor(name, list(shape), F32, kind="ExternalInput").ap()

    def dout(name, shape):
        return nc.dram_tensor(name, list(shape), F32, kind="ExternalOutput").ap()

    x_p = din("x_p", [NTP * 128, D])
    x_s = din("x_s", [2, 32, D])
    ck_d = din("ck", [2, 128, 256])
    cv_d = din("cv", [2, 128, 256])
    sconv_d = din("sconv", [2, 128, 72])
    sssm_d = din("sssm", [2, 128, 2048])
    w_in = din("w_in", [D, INW])
    w_ao = din("w_ao", [1024, 1024])
    w_so = din("w_so", [2048, 1024])
    w_o = din("w_o", [1024, 1024])
    prew_d = din("prew", [128, 8])
    ssmw_d = din("ssmw", [128, 16])
    postw_d = din("postw", [1024])
    convw_d = din("convw", [128, 96])
    convb_d = din("convb", [128, 24])
    dtb_d = din("dtb", [32])
    alog_d = din("alog", [32])
    dsk_d = din("dsk", [32])
    sinks_d = din("sinks", [16])
    cos_d = din("cos2", [128, NTAB * 16])
    sin_d = din("sin2", [128, NTAB * 16])
    U_d = din("Umat", [128, 128])
    NEG_d = din("NEGmat", [128, 128])
    id_d = din("ident", [128, 128])

    y_p = dout("y_p", [NTP * 128, D])
    y_s = dout("y_s", [2, 32, D])
    kp_o = dout("kp_o", [128, 256])
    vp_o = dout("vp_o", [128, 256])
    convp_o = dout("convp_o", [128, 72])
    ssmp_o = dout("ssmp_o", [128, 2048])
    ks_o = dout("ks_o", [2, 32, 256])
    vs_o = dout("vs_o", [2, 32, 256])
    convs_o = dout("convs_o", [2, 128, 72])
    ssms_o = dout("ssms_o", [2, 128, 2048])
    cumscr = nc.dram_tensor("cumscr", [4, 32, 128], F32, kind="Internal").ap()

    import contextlib
    es = contextlib.ExitStack()
    with es:
        def sb(name, free, dt=F32):
            return TT(es.enter_context(nc.sbuf_tensor(name, [128, free], dt)))

        identb = sb("identb", 128, BF16)
        identf = sb("identf", 128)
        Umat = sb("Umat_s", 128)
        ones = sb("ones_s", 128)
        NEGm = sb("NEG_s", 128)
        prew = sb("prew_s", 8)
        ssmw = sb("ssmw_s", 16)
        postw = sb("postw_s", 1024)
        convw = sb("convw_s", 96)
        convb = sb("convb_s", 24)
        dtb = sb("dtb_s", 32)
        a_b = sb("a_s", 32)
        dsk = sb("dsk_s", 32)
        esink = sb("esink_s", 16)
        NSLOT = 3
        wslot = [sb(f"wslot{i}", 4096, BF16) for i in range(NSLOT)]
        xin = [sb(f"xin{i}", 1024) for i in range(2)]
        junk = sb("junk", 1024, BF16)
        hbf = sb("hbf", 1024, BF16)
        hT = sb("hT", 8 * 512, BF16)
        t1m = sb("t1m", 8 * 512, BF16)
        state = sb("state", 2048)
        statebf = sb("statebf", 2048, BF16)
        Vext = [sb(f"vext{i}", 4 * 65, BF16) for i in range(5)]
        Vc = [sb(f"vc{i}", 4 * 65, BF16) for i in range(2)]
        kTr = [sb(f"kT{i}", 512, BF16) for i in range(2)]
        kTc = [sb(f"kTc{i}", 512, BF16) for i in range(2)]
        small = sb("small", 32 * 32)
        arena_h = es.enter_context(nc.sbuf_tensor("arena", [128, 12288], F32))
        arena_f = arena_h[:]
        arena_b = arena_h.bitcast(BF16)[:]
        _ar = {"p1": 0, "p2": 0, "p3": 0}

        def ar(ph, nbytes, dt):
            o = _ar[ph]
            _ar[ph] = o + nbytes
            assert _ar[ph] <= 49152, (ph, _ar[ph])
            if dt == F32:
                return TT(arena_f[:, o // 4:(o + nbytes) // 4])
            return TT(arena_b[:, o // 2:(o + nbytes) // 2])

        q_bf = [ar("p1", 2048, BF16) for i in range(4)]
        sg_bf = [ar("p1", 2048, BF16) for i in range(4)]
        qT = ar("p1", 4096, BF16)
        PT = [[ar("p1", 1024, BF16) for i in range(2)] for a in range(2)]
        attn_f = ar("p1", 4096, F32)
        attg = ar("p1", 2048, BF16)
        attgT = ar("p1", 8192, BF16)
        k_bf = [ar("p1", 512, BF16) for i in range(4)]
        kf = ar("p1", 1024, F32)
        vf = ar("p1", 1024, F32)
        ropeA = ar("p1", 1024, F32)
        ropeB = ar("p1", 1024, F32)
        sz_bf = [ar("p2", 4096, BF16) for i in range(4)]
        xc = [ar("p2", 4096, BF16) for i in range(4)]
        yzT = ar("p2", 16384, BF16)
        outf = [ar("p3", 4096, F32) for i in range(4)]
        ytmp = ar("p3", 4096, F32)
        xr = [TT(arena_f[:, 24576 // 4:(24576 + 4096) // 4]), TT(arena_f[:, 30720 // 4:(30720 + 4096) // 4])]
        pjunk = TT(arena_b[:, 34816 // 2:(34816 + 2048) // 2])
        gsig = sb("gsig", 4 * 512, BF16)
        csg = sb("csg", 128)
        snn = sb("snn", 128)
        arena2_h = es.enter_context(nc.sbuf_tensor("arena2", [128, 4096], F32))
        arena2_f = arena2_h[:]
        arena2_b = arena2_h.bitcast(BF16)[:]
        _ar2 = {"c": 0, "s": 0}

        def ar2(ph, nbytes, dt):
            o = _ar2[ph]
            _ar2[ph] = o + nbytes
            assert _ar2[ph] <= 16384, (ph, _ar2[ph])
            if dt == F32:
                return TT(arena2_f[:, o // 4:(o + nbytes) // 4])
            return TT(arena2_b[:, o // 2:(o + nbytes) // 2])

        raw_sb = [ar2("c", 2080, F32) for i in range(2)]
        acc = [ar2("c", 2048, F32) for i in range(2)]
        xcfm = [ar2("c", 1024, BF16) for i in range(4)]
        MTa = [ar2("s", 8192, BF16) for i in range(2)]
        tv = [sb(f"tv{i}", 8 * 32) for i in range(2)]
        BT = sb("BT", 4 * 512, BF16)
        CT = sb("CT", 4 * 512, BF16)
        Btok = [sb(f"Btok{i}", 512, BF16) for i in range(4)]
        dtraw = [sb(f"dtraw{i}", 32) for i in range(4)]
        tailb = sb("tailb", 72)
        convo = [sb(f"convo{i}", 72) for i in range(2)]
        sconv = [sb(f"sconv{i}", 72) for i in range(2)]
        cumTs = sb("cumTs", 128)
        cumB = [sb(f"cumB{i}", 1024) for i in range(2)]
        Ldt = [sb(f"Ldt{i}", 1024, BF16) for i in range(2)]
        yall = sb("yall", 2048)
        xD = [sb("xD0", 512)]
        yzna = sb("yzna", 2048, BF16)
        xw = sb("xw", 2048, BF16)

        psum = es.enter_context(nc.psum_tensor("psum", [128, 8 * 512], F32))
        PS = TT(psum)
        PSB = TT(psum.bitcast(BF16))

        sems = {e: es.enter_context(nc.semaphore(f"sem_{e}")) for e in Tracker.ENG}
        dma_sems = [es.enter_context(nc.semaphore(f"dsem{i}")) for i in range(48)]
        T = Tracker(nc, sems, dma_sems)
        T.set_phase([f"q_bf{i}" for i in range(4)] + [f"sg_bf{i}" for i in range(4)] + ["qT", "PT00", "PT01", "PT10", "PT11", "attn_f", "attg", "attgT"]
                    + [f"k_bf{i}" for i in range(4)] + ["kf", "vf", "ropeA", "ropeB"], "A1p1")
        T.set_phase([f"sz_bf{i}" for i in range(4)] + [f"xc{i}" for i in range(4)] + [f"yzT{k}" for k in range(16)], "A1p2")
        T.set_phase([f"outf{i}" for i in range(4)] + ["ytmp", "xr0", "xr1", "pjunk"], "A1p3")
        T.set_phase(["raw0", "raw1", "acc0", "acc1", "xcfm0", "xcfm1", "xcfm2", "xcfm3"], "A2c")
        T.set_phase([f"MT{p}_{g}" for p in range(2) for g in range(4)], "A2s")

        def psb(b, off=0):
            return b * 512 + off

        def full(t, n, off=0, p0=0, p1=128):
            return t.v(p0, p1, off, [[1, n]])

        _sm = [0]

        def sm(n=32):
            i = _sm[0]
            _sm[0] = (i + 1) % 32
            return ("small%d" % i, lambda p0=0, p1=128, w=n, ii=i: small.v(p0, p1, ii * 32, [[1, w]]))

        def ld(q, t, n, src, name, **kw):
            T.dma(q, full(t, n), src, W=[name], **kw)

        ld("sp", identf, 128, id_d, "identf")
        ld("sp", Umat, 128, U_d, "Umat")
        ld("sp", NEGm, 128, NEG_d, "NEG")
        ld("sp", prew, 8, prew_d, "prew")
        ld("sp", ssmw, 16, ssmw_d, "ssmw")
        ld("sp", postw, 1024, postw_d.partition_broadcast(128), "postw")
        ld("sp", convw, 96, convw_d, "convw")
        ld("sp", convb, 24, convb_d, "convb")
        ld("sp", dtb, 32, dtb_d.partition_broadcast(128), "dtb")
        ld("sp", a_b, 32, alog_d.partition_broadcast(128), "a_b")
        ld("sp", dsk, 32, dsk_d.partition_broadcast(128), "dsk")
        ld("sp", esink, 16, sinks_d.partition_broadcast(128), "esink")
        T.op("dve", C("tensor_copy", out=full(identb, 128), in_=full(identf, 128)), R=["identf"], W=["identb"])
        T.op("dve", C("memset", full(ones, 128), 1.0), W=["ones"])
        T.op("act", C("activation", out=full(a_b, 32), in_=full(a_b, 32), func=AF.Exp), R=["a_b"], W=["a_b"])
        T.op("dve", C("tensor_scalar", out=full(a_b, 32), in0=full(a_b, 32), scalar1=-1.0, scalar2=None, op0=ALU.mult), R=["a_b"], W=["a_b"])
        T.op("act", C("activation", out=full(esink, 16), in_=full(esink, 16), func=AF.Exp), R=["esink"], W=["esink"])
        T.op("dve", C("memset", full(state, 2048), 0.0), W=["state", "state0", "state1"])
        T.op("dve", C("memset", full(statebf, 2048), 0.0), W=["statebf"])
        T.op("dve", C("memset", full(tailb, 72), 0.0), W=["tailb"])
        for i in range(5):
            T.op("pool", C("memset", full(Vext[i], 260), 1.0), W=[f"vext{i}"])
        for i in range(2):
            T.op("pool", C("memset", full(Vc[i], 260), 1.0), W=[f"vc{i}"])
            for a in range(2):
                T.op("pool", C("memset", full(PT[a][i], 512), 0.0), W=[f"PT{a}{i}"])
            T.op("pool", C("memset", full(xin[i], 1024), 0.0), W=[f"xin{i}"])

        def w_in_chunk(c0, nco):
            return (w_in.rearrange("(kt p) c -> p kt c", p=128)[:, :, c0:c0 + nco], 8, nco)

        def w_chunk(w, nkt, c0, nco):
            return (w.rearrange("(kt p) c -> p kt c", p=128)[:, :, c0:c0 + nco], nkt, nco)

        chunk_list = []
        chunk_list += [("q", w_in_chunk(0, 512)), ("q", w_in_chunk(512, 512)), ("kv", w_in_chunk(1024, 512)),
                       ("g", w_in_chunk(1536, 512)), ("g", w_in_chunk(2048, 512))]
        chunk_list += [("ga", w_in_chunk(7712, 512)), ("ao", w_chunk(w_ao, 8, 0, 512)),
                       ("ga", w_in_chunk(8224, 512)), ("ao", w_chunk(w_ao, 8, 512, 512))]
        chunk_list += [("z", w_in_chunk(2560 + 512 * i, 512)) for i in range(4)]
        chunk_list += [("xbc", w_in_chunk(4608 + 512 * i, 512)) for i in range(6)]
        chunk_list += [("dt", w_in_chunk(7680, 32))]
        for i in range(2):
            chunk_list += [("gs", w_in_chunk(8736 + 512 * i, 512)),
                           ("so", w_chunk(w_so, 16, 512 * i, 256)), ("so", w_chunk(w_so, 16, 512 * i + 256, 256))]
        chunk_list += [("wo", w_chunk(w_o, 8, 0, 512)), ("wo", w_chunk(w_o, 8, 512, 512))]
        NCH = len(chunk_list)
        total_groups = NG + (1 if with_sample else 0)
        wstate = {"loaded": 0, "used": 0}

        def w_prefetch(upto):
            while wstate["loaded"] <= upto and wstate["loaded"] < NCH * total_groups:
                gidx = wstate["loaded"]
                kind, (src, nkt, nco) = chunk_list[gidx % NCH]
                s = gidx % NSLOT
                dst = wslot[s].v(0, 128, 0, [[nco, nkt], [1, nco]])
                T.dma("pool", dst, src, W=[f"wslot{s}"])
                wstate["loaded"] += 1

        def w_next(expect):
            gidx = wstate["used"]
            kind, (src, nkt, nco) = chunk_list[gidx % NCH]
            assert kind == expect, (kind, expect)
            w_prefetch(gidx + NSLOT - 1)
            wstate["used"] += 1
            s = gidx % NSLOT
            return s, nkt, nco

        def wv(s, nkt, nco, kt, c0, n):
            return wslot[s].v(0, 128, kt * nco + c0, [[1, n]])

        def rstd_from_ss(ss_name, ss_ap, n, width=1):
            nm, f = sm()
            T.op("dve", C("tensor_scalar", out=f(w=width), in0=ss_ap, scalar1=1.0 / n, scalar2=1e-6, op0=ALU.mult, op1=ALU.add), R=[ss_name], W=[nm])
            T.op("act", C("activation", out=f(w=width), in_=f(w=width), func=AF.Ln), R=[nm], W=[nm])
            T.op("act", C("activation", out=f(w=width), in_=f(w=width), func=AF.Exp, scale=-0.5), R=[nm], W=[nm])
            return nm, f

        def p0_emit(tiles):
            is_s = tiles[0]["kind"] == "s"
            for ti, tl in enumerate(tiles):
                xb = ti % 2
                if is_s:
                    T.dma("sp", full(xin[xb], 1024, p1=32), x_s[tl["s"]], W=[f"xin{xb}"])
                else:
                    gi = tl["gi"]
                    T.dma("sp", full(xin[xb], 1024), x_p[gi * 128:(gi + 1) * 128, :], W=[f"xin{xb}"])
                ssn, ssf = sm()
                T.op("act", C("activation", out=full(junk, 1024), in_=full(xin[xb], 1024), func=AF.Square, accum_out=ssf(w=1)),
                     R=[f"xin{xb}"], W=["junk", ssn])
                rn, rf = rstd_from_ss(ssn, ssf(w=1), 1024.0)
                T.op("dve", C("tensor_scalar", out=full(hbf, 1024), in0=full(xin[xb], 1024), scalar1=rf(w=1), scalar2=None, op0=ALU.mult),
                     R=[f"xin{xb}", rn], W=["hbf"])
                for kt in range(8):
                    T.op("pe", C("transpose", out=PSB.v(0, 128, 2 * psb(2) + kt * 128, [[1, 128]]), in_=hbf.v(0, 128, kt * 128, [[1, 128]]), identity=full(identb, 128)),
                         R=["hbf", "identb"], W=["ps2"])
                for kt in range(8):
                    T.op("act", C("activation", out=hT.v(0, 128, kt * 512 + ti * 128, [[1, 128]]), in_=PSB.v(0, 128, 2 * psb(2) + kt * 128, [[1, 128]]),
                                                                      func=AF.Copy, scale=prew.v(0, 128, kt, [[1, 1]])),
                         R=["ps2", "prew"], W=["hT"])

        def post_emit(tiles):
            is_s = tiles[0]["kind"] == "s"
            for ti, tl in enumerate(tiles):
                xb = ti % 2
                if is_s:
                    T.dma("sp", full(xr[xb], 1024, p1=32), x_s[tl["s"]], W=[f"xr{xb}"])
                else:
                    gi = tl["gi"]
                    T.dma("sp", full(xr[xb], 1024), x_p[gi * 128:(gi + 1) * 128, :], W=[f"xr{xb}"])
                ssn, ssf = sm()
                T.op("act", C("activation", out=full(pjunk, 1024), in_=full(outf[ti], 1024), func=AF.Square, accum_out=ssf(w=1)), R=[f"outf{ti}"], W=["pjunk", ssn])
                rn, rf = rstd_from_ss(ssn, ssf(w=1), 1024.0)
                T.op("dve", C("scalar_tensor_tensor", out=full(ytmp, 1024), in0=full(outf[ti], 1024), scalar=rf(w=1), in1=full(postw, 1024), op0=ALU.mult, op1=ALU.mult),
                     R=[f"outf{ti}", rn, "postw"], W=["ytmp"])
                T.op("dve", C("tensor_tensor", out=full(ytmp, 1024), in0=full(ytmp, 1024), in1=full(xr[xb], 1024), op=ALU.add), R=["ytmp", f"xr{xb}"], W=["ytmp"])
                if is_s:
                    T.dma("sp", y_s[tl["s"]], full(ytmp, 1024, p1=32), R=["ytmp"], is_out=True)
                else:
                    gi = tl["gi"]
                    T.dma("sp", y_p[gi * 128:(gi + 1) * 128, :], full(ytmp, 1024), R=["ytmp"], is_out=True)

        def group(g0, tiles):
            NT = len(tiles)
            NTOK = NT * 128
            is_s = tiles[0]["kind"] == "s"

            if is_s and False:
                pass

            stage(1)
            def formB(s, nkt, nco, ti, bank, n=None, c0=0):
                n = nco if n is None else n
                for kt in range(nkt):
                    T.op("pe", C("matmul", PS.v(0, 128, psb(bank), [[1, n]]), hT.v(0, 128, kt * 512 + ti * 128, [[1, 128]]), wv(s, nkt, nco, kt, c0, n),
                                                         start=(kt == 0), stop=(kt == nkt - 1)),
                         R=["hT", f"wslot{s}"], W=[f"ps{bank}"])

            def formA(s, nkt, nco, ct, bank, rhsT, rname, rstride=512, per_kt=False):
                for kt in range(nkt):
                    rn_ = f"{rname}{kt}" if per_kt else rname
                    T.op("pe", C("matmul", PS.v(0, 128, psb(bank), [[1, NTOK]]), wv(s, nkt, nco, kt, ct * 128, 128), rhsT.v(0, 128, kt * rstride, [[1, NTOK]]),
                                                         start=(kt == 0), stop=(kt == nkt - 1)),
                         R=[rn_, f"wslot{s}"], W=[f"ps{bank}"])

            if is_s:
                T.dma("sp", csg.v(0, 128, 0, [[1, 16]]), cos_d[:, NTP * 16:NTP * 16 + 16], W=["csg"])
                T.dma("sp", snn.v(0, 128, 0, [[1, 16]]), sin_d[:, NTP * 16:NTP * 16 + 16], W=["snn"])
            else:
                T.dma("sp", csg.v(0, 128, 0, [[1, 64]]), cos_d[:, g0 * 64:g0 * 64 + 64], W=["csg"])
                T.dma("sp", snn.v(0, 128, 0, [[1, 64]]), sin_d[:, g0 * 64:g0 * 64 + 64], W=["snn"])

            def tabidx(tl):
                return 0 if tl["kind"] == "s" else tl["gi"] % 4

            def rope(src_bank, nheads, tl, dst, dname, dst_hstride, perm=False):
                ti_ = tabidx(tl)
                cs = csg.v(0, 128, ti_ * 16, [[0, nheads], [1, 16]])
                sn = snn.v(0, 128, ti_ * 16, [[0, nheads], [1, 16]])
                src = PS.v(0, 128, psb(src_bank), [[64, nheads], [1, 16]])
                T.op("dve", C("tensor_tensor", out=ropeA.v(0, 128, 0, [[16, nheads], [1, 16]]), in0=src, in1=cs, op=ALU.mult), R=[f"ps{src_bank}", "csg"], W=["ropeA"])
                stage(1.45)
                T.op("dve", C("tensor_tensor", out=ropeB.v(0, 128, 0, [[16, nheads], [1, 16]]), in0=src, in1=sn, op=ALU.mult), R=[f"ps{src_bank}", "snn"], W=["ropeB"])
                stage(1.5)
                if perm:
                    op_ = [[64, 2], [128, 4], [1, 8]]
                    ip_ = [[64, 2], [16, 4], [1, 8]]
                else:
                    op_ = [[dst_hstride, nheads], [1, 8]]
                    ip_ = [[16, nheads], [1, 8]]
                T.op("dve", C("tensor_tensor", out=dst.v(0, 128, 0, op_), in0=ropeA.v(0, 128, 0, ip_),
                                                      in1=ropeB.v(0, 128, 8, ip_), op=ALU.subtract), R=["ropeA", "ropeB"], W=[dname])
                stage(1.55)
                T.op("dve", C("tensor_tensor", out=dst.v(0, 128, 8, op_), in0=ropeA.v(0, 128, 8, ip_),
                                                      in1=ropeB.v(0, 128, 0, ip_), op=ALU.add), R=["ropeA", "ropeB"], W=[dname])

            for c in range(2):
                s, nkt, nco = w_next("q")
                for ti, tl in enumerate(tiles):
                    bank = ti % 2
                    stage(1.1)
                    formB(s, nkt, nco, ti, bank)
                    stage(1.2)
                    T.op("act", C("activation", out=q_bf[ti].v(0, 128, c * 512, [[64, 2], [128, 4], [1, 64]]), in_=PS.v(0, 128, psb(bank), [[256, 2], [64, 4], [1, 64]]), func=AF.Copy),
                         R=[f"ps{bank}"], W=[f"q_bf{ti}"])
                    stage(1.4)
                    rope(bank, 8, tl, TT(q_bf[ti].v(0, 128, c * 512, [[1, 512]])), f"q_bf{ti}", 64, perm=True)
                    stage(1.6)
            s, nkt, nco = w_next("kv")
            for ti, tl in enumerate(tiles):
                bank = ti % 2
                vslot = (g0 * 4 + ti) % 5 if not is_s else None
                formB(s, nkt, nco, ti, bank)
                T.op("act", C("activation", out=full(kf, 256), in_=PS.v(0, 128, psb(bank), [[1, 256]]), func=AF.Copy), R=[f"ps{bank}"], W=["kf"])
                rope(bank, 4, tl, kf, "kf", 64)
                T.op("act", C("activation", out=full(k_bf[ti], 256), in_=full(kf, 256), func=AF.Copy), R=["kf"], W=[f"k_bf{ti}"])
                if is_s:
                    vt, vn = Vext[ti], f"vext{ti}"
                else:
                    vt, vn = Vext[vslot], f"vext{vslot}"
                tl["vt"], tl["vn"] = vt, vn
                T.op("dve", C("tensor_copy", out=vt.v(0, 128, 0, [[65, 4], [1, 64]]), in_=PS.v(0, 128, psb(bank, 256), [[64, 4], [1, 64]])),
                     R=[f"ps{bank}"], W=[vn])
                want_out = is_s or (tl["gi"] == NTP - 1)
                if want_out:
                    T.op("act", C("activation", out=full(vf, 256), in_=PS.v(0, 128, psb(bank, 256), [[1, 256]]), func=AF.Copy), R=[f"ps{bank}"], W=["vf"])
                    if is_s:
                        T.dma("sp", ks_o[tl["s"]], full(kf, 256, p1=32), R=["kf"], is_out=True)
                        T.dma("sp", vs_o[tl["s"]], full(vf, 256, p1=32), R=["vf"], is_out=True)
                    else:
                        T.dma("sp", kp_o, full(kf, 256), R=["kf"], is_out=True)
                        T.dma("sp", vp_o, full(vf, 256), R=["vf"], is_out=True)
            for c in range(2):
                s, nkt, nco = w_next("g")
                for ti, tl in enumerate(tiles):
                    bank = ti % 2
                    formB(s, nkt, nco, ti, bank)
                    T.op("act", C("activation", out=sg_bf[ti].v(0, 128, c * 512, [[1, 512]]), in_=PS.v(0, 128, psb(bank), [[1, 512]]), func=AF.Silu),
                         R=[f"ps{bank}"], W=[f"sg_bf{ti}"])

            stage(2)
            if not is_s:
                for i in range(2):
                    T.op("pool", C("memset", PT[0][i].v(0, 64, 64, [[128, 4], [1, 64]]), 0.0), W=[f"PT0{i}"])
                    T.op("pool", C("memset", PT[1][i].v(64, 128, 0, [[128, 4], [1, 64]]), 0.0), W=[f"PT1{i}"])
            for ti, tl in enumerate(tiles):
                for t in range(8):
                    T.op("pe", C("transpose", out=PSB.v(0, 128, 2 * psb(2) + t * 128, [[1, 128]]),
                                 in_=q_bf[ti].v(0, 128, t * 128, [[1, 128]]), identity=full(identb, 128)),
                         R=[f"q_bf{ti}", "identb"], W=["ps2"])
                T.op("act", C("activation", out=qT.v(0, 128, 0, [[1, 1024]]), in_=PSB.v(0, 128, 2 * psb(2), [[1, 1024]]), func=AF.Copy),
                     R=["ps2"], W=["qT"])
                stage(2.2)
                if is_s:
                    kcur, kcn = kTr[0], "kT0"
                    kprev, kpn = kTc[ti], f"kTc{ti}"
                    vprev, vpn = Vc[ti], f"vc{ti}"
                    T.dma("pool", full(kTc[ti], 256), ck_d[tl["s"]], W=[kpn])
                    T.dma("pool", Vc[ti].v(0, 128, 0, [[65, 4], [1, 64]]), cv_d[tl["s"]].rearrange("p (h d) -> p h d", h=4), W=[vpn])
                    hasA = True
                    nkB = 32
                else:
                    gi = tl["gi"]
                    kcur, kcn = kTr[gi % 2], f"kT{gi % 2}"
                    kprev, kpn = kTr[(gi + 1) % 2], f"kT{(gi + 1) % 2}"
                    pv = (gi - 1) % 5
                    vprev, vpn = Vext[pv], f"vext{pv}"
                    hasA = gi > 0
                    nkB = 128
                for u in range(2):
                    T.op("pe", C("transpose", out=PSB.v(0, 128, 2 * psb(6) + u * 128, [[1, 128]]), in_=k_bf[ti].v(0, 128, u * 128, [[1, 128]]), identity=full(identb, 128)),
                         R=[f"k_bf{ti}", "identb"], W=["ps6"])
                T.op("act", C("activation", out=full(kcur, 256), in_=PSB.v(0, 128, 2 * psb(6), [[1, 256]]), func=AF.Copy), R=["ps6"], W=[kcn])
                stage(2.4)
                vcur, vcn = tl["vt"], tl["vn"]
                def att_chain(hk):
                    bA = 2 if hk % 2 == 0 else 0
                    bB = 3 if hk % 2 == 0 else 1
                    pa = PT[0][hk % 2]
                    pan = f"PT0{hk % 2}"
                    pb_ = PT[1][hk % 2]
                    pbn = f"PT1{hk % 2}"
                    kp0 = (hk % 2) * 64
                    ku = hk // 2
                    qrhs = qT.v(kp0, kp0 + 64, (0 if hk < 2 else 4) * 128, [[1, 512]])
                    if hasA:
                        T.op("pe", C("matmul", PS.v(0, 128, psb(bA), [[1, 512]]), kprev.v(kp0, kp0 + 64, ku * 128, [[1, 128]]), qrhs, start=True, stop=True),
                             R=[kpn, "qT"], W=[f"ps{bA}"])
                        if is_s:
                            regs = [(0, 128, 0, 128)]
                        else:
                            regs = [(0, 128, 0, 64), (64, 128, 64, 64)]
                        for (p0, p1, q0, nq) in regs:
                            T.op("act", C("activation", out=pa.v(p0, p1, q0, [[128, 4], [1, nq]]), in_=PS.v(p0, p1, psb(bA, q0), [[128, 4], [1, nq]]),
                                                                                                 func=AF.Exp, scale=0.125), R=[f"ps{bA}"], W=[pan])
                    T.op("pe", C("matmul", PS.v(0, 128, psb(bB), [[1, 512]]), kcur.v(kp0, kp0 + 64, ku * 128, [[1, 128]]), qrhs, start=True, stop=True),
                         R=[kcn, "qT"], W=[f"ps{bB}"])
                    if is_s:
                        regs = [(0, 32, 0, 128)]
                    else:
                        regs = [(0, 64, 0, 128), (64, 128, 64, 64)]
                    for (p0, p1, q0, nq) in regs:
                        T.op("act", C("activation", out=pb_.v(p0, p1, q0, [[128, 4], [1, nq]]), in_=PS.v(p0, p1, psb(bB, q0), [[128, 4], [1, nq]]),
                                                                                               func=AF.Exp, scale=0.125), R=[f"ps{bB}"], W=[pbn])
                    for r in range(4):
                        h = hk * 4 + r
                        obank = 4 + hk % 2
                        ooff = psb(obank, ((hk // 2) * 4 + r) * 64)
                        sbank = 7 if hk % 2 == 0 else 6
                        soff = psb(sbank, h * 2)
                        if hasA:
                            T.op("pe", C("matmul", PS.v(0, 128, ooff, [[1, 64]]), pa.v(0, 128, r * 128, [[1, 128]]), vprev.v(0, 128, hk * 65, [[1, 64]]),
                                                                                                    start=True, stop=False), R=[pan, vpn], W=[f"ps{obank}"])
                            T.op("pe", C("matmul", PS.v(0, 128, soff, [[1, 2]]), pa.v(0, 128, r * 128, [[1, 128]]), onesb.v(0, 128, 0, [[1, 2]]),
                                                                                                    start=True, stop=False), R=[pan, "onesb"], W=[f"ps{sbank}"])
                        T.op("pe", C("matmul", PS.v(0, 128, ooff, [[1, 64]]), pb_.v(0, nkB, r * 128, [[1, 128]]), vcur.v(0, nkB, hk * 65, [[1, 64]]),
                                                                                                start=not hasA, stop=True), R=[pbn, vcn], W=[f"ps{obank}"])
                        T.op("pe", C("matmul", PS.v(0, 128, soff, [[1, 2]]), pb_.v(0, nkB, r * 128, [[1, 128]]), onesb.v(0, nkB, 0, [[1, 2]]),
                                                                              start=not hasA, stop=True), R=[pbn, "onesb"], W=[f"ps{sbank}"])
                recs = [T.rec(att_chain, hk) for hk in range(4)]
                T.play(recs[0:2])
                T.play(recs[2:4])
                stage(2.8)
                rn, rf = sm()
                ridx = int(rn[5:])
                for par in range(2):
                    sbank = 7 if par == 0 else 6
                    T.op("dve", C("tensor_tensor", out=small.v(0, 128, ridx * 32 + par * 4, [[8, 2], [1, 4]]), in0=PS.v(0, 128, psb(sbank, par * 8), [[16, 2], [2, 4]]),
                                  in1=esink.v(0, 128, par * 4, [[8, 2], [1, 4]]), op=ALU.add), R=[f"ps{sbank}", "esink"], W=[rn])
                T.op("dve", C("reciprocal", out=rf(w=16), in_=rf(w=16)), R=[rn], W=[rn])
                for par in range(2):
                    bank = 4 + par
                    T.op("dve", C("tensor_tensor", out=attn_f.v(0, 128, par * 256, [[512, 2], [64, 4], [1, 64]]), in0=PS.v(0, 128, psb(bank), [[256, 2], [64, 4], [1, 64]]),
                                  in1=small.v(0, 128, ridx * 32 + par * 4, [[8, 2], [1, 4], [0, 64]]), op=ALU.mult),
                         R=[f"ps{bank}", rn], W=["attn_f"])
                T.op("dve", C("tensor_tensor", out=full(attg, 1024), in0=full(attn_f, 1024), in1=full(sg_bf[ti], 1024), op=ALU.mult), R=["attn_f", f"sg_bf{ti}"], W=["attg"])
                stage(2.9)
                for kt in range(8):
                    T.op("pe", C("transpose", out=PSB.v(0, 128, 2 * psb(6) + kt * 128, [[1, 128]]), in_=attg.v(0, 128, kt * 128, [[1, 128]]), identity=full(identb, 128)),
                         R=["attg", "identb"], W=["ps6"])
                T.op("act", C("activation", out=attgT.v(0, 128, ti * 128, [[512, 8], [1, 128]]), in_=PSB.v(0, 128, 2 * psb(6), [[128, 8], [1, 128]]), func=AF.Copy),
                     R=["ps6"], W=["attgT"])

            stage(3)
            for c in range(2):
                s, nkt, nco = w_next("ga")
                for ct in range(4):
                    bank = ct % 2
                    formA(s, nkt, nco, ct, bank, hT, "hT")
                    T.op("act", C("activation", out=gsig.v(0, 128, ct * 512, [[1, NTOK]]), in_=PS.v(0, 128, psb(bank), [[1, NTOK]]), func=AF.Sigmoid),
                         R=[f"ps{bank}"], W=["gsig"])
                s, nkt, nco = w_next("ao")
                for ct in range(4):
                    bank = ct % 2
                    formA(s, nkt, nco, ct, bank, attgT, "attgT")
                    T.op("dve", C("tensor_tensor", out=t1m.v(0, 128, (c * 4 + ct) * 512, [[1, NTOK]]), in0=PS.v(0, 128, psb(bank), [[1, NTOK]]),
                                                                                in1=gsig.v(0, 128, ct * 512, [[1, NTOK]]), op=ALU.mult), R=[f"ps{bank}", "gsig"], W=["t1m"])

            stage(4)
            for c in range(4):
                s, nkt, nco = w_next("z")
                for ti, tl in enumerate(tiles):
                    bank = ti % 2
                    formB(s, nkt, nco, ti, bank)
                    T.op("act", C("activation", out=sz_bf[ti].v(0, 128, c * 512, [[1, 512]]), in_=PS.v(0, 128, psb(bank), [[1, 512]]), func=AF.Silu),
                         R=[f"ps{bank}"], W=[f"sz_bf{ti}"])
            stage(5)
            if is_s:
                for si, tl in enumerate(tiles):
                    T.dma("sp", full(sconv[si], 72), sconv_d[tl["s"]], W=[f"sconv{si}"])
            for c in range(6):
                s, nkt, nco = w_next("xbc")
                def conv_chain(c, ct, s, nkt, nco):
                    j = c * 4 + ct
                    bank = ct % 2
                    rb = j % 2
                    formA(s, nkt, nco, ct, bank, hT, "hT")
                    rw, rwn = raw_sb[rb], f"raw{rb}"
                    T.op("act", C("activation", out=rw.v(0, 128, 3, [[1, NTOK]]), in_=PS.v(0, 128, psb(bank), [[1, NTOK]]), func=AF.Copy), R=[f"ps{bank}"], W=[rwn])
                    if is_s:
                        T.op("pool", C("tensor_copy", out=rw.v(0, 128, 0, [[1, 3]]), in_=sconv[0].v(0, 128, j * 3, [[1, 3]])), R=["sconv0"], W=[rwn])
                        if NT > 1:
                            T.op("pool", C("tensor_copy", out=rw.v(0, 128, 3 + 125, [[1, 3]]), in_=sconv[1].v(0, 128, j * 3, [[1, 3]])), R=["sconv1"], W=[rwn])
                        for si in range(NT):
                            T.op("pool", C("tensor_copy", out=convo[si].v(0, 128, j * 3, [[1, 3]]), in_=rw.v(0, 128, 3 + si * 128 + 29, [[1, 3]])), R=[rwn], W=[f"convo{si}"])
                    else:
                        T.op("pool", C("tensor_copy", out=rw.v(0, 128, 0, [[1, 3]]), in_=tailb.v(0, 128, j * 3, [[1, 3]])), R=["tailb"], W=[rwn])
                        T.op("pool", C("tensor_copy", out=tailb.v(0, 128, j * 3, [[1, 3]]), in_=rw.v(0, 128, NTOK, [[1, 3]])), R=[rwn], W=["tailb"])
                    ac, acn = acc[rb], f"acc{rb}"
                    T.op("act", C("activation", out=ac.v(0, 128, 0, [[1, NTOK]]), in_=rw.v(0, 128, 3, [[1, NTOK]]), func=AF.Identity,
                                                                        scale=convw.v(0, 128, j * 4 + 3, [[1, 1]]), bias=convb.v(0, 128, j, [[1, 1]])), R=[rwn, "convw", "convb"], W=[acn])
                    for tap in (2, 1, 0):
                        T.op("dve", C("scalar_tensor_tensor", out=ac.v(0, 128, 0, [[1, NTOK]]), in0=rw.v(0, 128, tap, [[1, NTOK]]),
                                                                                               scalar=convw.v(0, 128, j * 4 + tap, [[1, 1]]), in1=ac.v(0, 128, 0, [[1, NTOK]]),
                                                                                               op0=ALU.mult, op1=ALU.add), R=[rwn, acn, "convw"], W=[acn])
                    if j < 16:
                        T.op("act", C("activation", out=xcfm[ct].v(0, 128, 0, [[1, NTOK]]), in_=ac.v(0, 128, 0, [[1, NTOK]]), func=AF.Silu), R=[acn], W=[f"xcfm{ct}"])
                    elif j < 20:
                        T.op("act", C("activation", out=BT.v(0, 128, (j - 16) * 512, [[1, NTOK]]), in_=ac.v(0, 128, 0, [[1, NTOK]]), func=AF.Silu), R=[acn], W=["BT"])
                    else:
                        T.op("act", C("activation", out=CT.v(0, 128, (j - 20) * 512, [[1, NTOK]]), in_=ac.v(0, 128, 0, [[1, NTOK]]), func=AF.Silu), R=[acn], W=["CT"])
                recs = [T.rec(conv_chain, c, ct, s, nkt, nco) for ct in range(4)]
                T.play(recs[0:2])
                T.play(recs[2:4])
                if c < 4:
                    for ti in range(NT):
                        bank = 2 + (ti // 2)
                        for ct in range(4):
                            T.op("pe", C("transpose", out=PSB.v(0, 128, 2 * psb(bank) + ((ti % 2) * 4 + ct) * 128, [[1, 128]]),
                                                                                     in_=xcfm[ct].v(0, 128, ti * 128, [[1, 128]]), identity=full(identb, 128)),
                                 R=[f"xcfm{ct}", "identb"], W=[f"ps{bank}"])
                    for ti in range(NT):
                        bank = 2 + (ti // 2)
                        T.op("dve", C("tensor_copy", out=xc[ti].v(0, 128, c * 512, [[1, 512]]), in_=PSB.v(0, 128, 2 * psb(bank) + (ti % 2) * 512, [[1, 512]])),
                             R=[f"ps{bank}"], W=[f"xc{ti}"])
                if c == 4:
                    for ti in range(NT):
                        bank = 2 + (ti // 2)
                        for g in range(4):
                            T.op("pe", C("transpose", out=PSB.v(0, 128, 2 * psb(bank) + ((ti % 2) * 4 + g) * 128, [[1, 128]]),
                                                                                   in_=BT.v(0, 128, g * 512 + ti * 128, [[1, 128]]), identity=full(identb, 128)),
                                 R=["BT", "identb"], W=[f"ps{bank}"])
                    for ti in range(NT):
                        bank = 2 + (ti // 2)
                        T.op("dve", C("tensor_copy", out=full(Btok[ti], 512), in_=PSB.v(0, 128, 2 * psb(bank) + (ti % 2) * 512, [[1, 512]])),
                             R=[f"ps{bank}"], W=[f"Btok{ti}"])
            if is_s:
                for si, tl in enumerate(tiles):
                    T.dma("sp", convs_o[tl["s"]], full(convo[si], 72), R=[f"convo{si}"], is_out=True)
            elif g0 == NG - 1:
                T.dma("sp", convp_o, full(tailb, 72), R=["tailb"], is_out=True)
            stage(6)
            s, nkt, nco = w_next("dt")
            for ti, tl in enumerate(tiles):
                bank = ti % 2
                formB(s, nkt, nco, ti, bank, n=32)
                T.op("dve", C("tensor_tensor", out=full(dtraw[ti], 32), in0=PS.v(0, 128, psb(bank), [[1, 32]]), in1=full(dtb, 32), op=ALU.add),
                     R=[f"ps{bank}", "dtb"], W=[f"dtraw{ti}"])

            stage(7)
            def tvv(p, k, w=32, p0=0, p1=128, off=0):
                return tv[p].v(p0, p1, k * 32 + off, [[1, w]])

            def ssd_A(ti, tl):
                p = ti % 2
                dtn, dan, nbn, ecn, ten, den = [f"tv{p}_{k}" for k in range(6)]
                T.op("act", C("activation", out=tvv(p, 0), in_=full(dtraw[ti], 32), func=AF.Exp), R=[f"dtraw{ti}"], W=[dtn])
                T.op("act", C("activation", out=tvv(p, 0), in_=tvv(p, 0), func=AF.Ln, bias=1.0), R=[dtn], W=[dtn])
                if is_s:
                    T.op("dve", C("memset", tvv(p, 0, p0=32, p1=64), 1e-30), W=[dtn])
                    T.op("dve", C("memset", tvv(p, 0, p0=64, p1=128), 1e-30), W=[dtn])
                T.op("dve", C("tensor_tensor", out=tvv(p, 1), in0=tvv(p, 0), in1=full(a_b, 32), op=ALU.mult), R=[dtn, "a_b"], W=[dan])
                T.op("pe", C("matmul", PS.v(0, 128, psb(7, 64), [[1, 32]]), full(Umat, 128), tvv(p, 1), start=True, stop=True), R=["Umat", dan], W=["ps7"])
                T.op("pe", C("matmul", PS.v(0, 128, psb(7, 96), [[1, 32]]), full(ones, 128), tvv(p, 1), start=True, stop=True), R=["ones", dan], W=["ps7"])
                T.op("dve", C("tensor_copy", out=dapad.v(0, 128, 0, [[1, 32]]), in_=tvv(p, 1)), R=[dan], W=["dapad"])
                T.op("pe", C("matmul", PS.v(0, 128, psb(7, 128), [[1, 128]]), full(dapad, 128), full(Umat, 128), start=True, stop=True), R=["Umat", "dapad"], W=["ps7"])
                T.op("act", C("activation", out=full(cumTs, 128, p1=32), in_=PS.v(0, 32, psb(7, 128), [[1, 128]]), func=AF.Copy), R=["ps7"], W=["cumTs"])
                T.dma("sp", cumscr[ti], full(cumTs, 128, p1=32), R=["cumTs"], W=[f"cumscr{ti}"])
                cum_ps = PS.v(0, 128, psb(7, 64), [[1, 32]])
                tot_ps = PS.v(0, 128, psb(7, 96), [[1, 32]])
                T.op("act", C("activation", out=tvv(p, 2), in_=tvv(p, 0), func=AF.Ln), R=[dtn], W=[nbn])
                T.op("dve", C("tensor_tensor", out=tvv(p, 2), in0=tvv(p, 2), in1=cum_ps, op=ALU.subtract), R=[nbn, "ps7"], W=[nbn])
                T.op("act", C("activation", out=tvv(p, 3), in_=cum_ps, func=AF.Exp), R=["ps7"], W=[ecn])
                T.op("act", C("activation", out=tvv(p, 4), in_=cum_ps, func=AF.Copy), R=["ps7"], W=[ten])
                T.op("dve", C("tensor_tensor", out=tvv(p, 4), in0=tot_ps, in1=tvv(p, 4), op=ALU.subtract), R=["ps7", ten], W=[ten])
                T.op("act", C("activation", out=tvv(p, 4), in_=tvv(p, 4), func=AF.Exp), R=[ten], W=[ten])
                T.op("dve", C("tensor_tensor", out=tvv(p, 4), in0=tvv(p, 4), in1=tvv(p, 0), op=ALU.mult), R=[ten, dtn], W=[ten])
                T.op("act", C("activation", out=tvv(p, 5), in_=tot_ps, func=AF.Exp), R=["ps7"], W=[den])
                for g in range(4):
                    T.op("pe", C("matmul", PS.v(0, 128, psb(6, g * 128), [[1, 128]]), BT.v(0, 128, g * 512 + ti * 128, [[1, 128]]), CT.v(0, 128, g * 512 + ti * 128, [[1, 128]]),
                                 start=True, stop=True), R=["BT", "CT"], W=["ps6"])
                for g in range(4):
                    cb = g % 2
                    T.dma("sp", full(cumB[cb], 1024), cumscr[ti, g * 8:(g + 1) * 8, :].rearrange("h i -> (h i)").partition_broadcast(128), R=[f"cumscr{ti}"], W=[f"cumB{cb}"])
                    T.op("pool", C("tensor_tensor", out=cumB[cb].v(0, 128, 0, [[128, 8], [1, 128]]), in0=cumB[cb].v(0, 128, 0, [[128, 8], [1, 128]]),
                                   in1=NEGm.v(0, 128, 0, [[0, 8], [1, 128]]), op=ALU.add), R=[f"cumB{cb}", "NEG"], W=[f"cumB{cb}"])
                    for hh in range(8):
                        h = g * 8 + hh
                        T.op("act", C("activation", out=Ldt[cb].v(0, 128, hh * 128, [[1, 128]]), in_=cumB[cb].v(0, 128, hh * 128, [[1, 128]]), func=AF.Exp,
                                      bias=tvv(p, 2, w=1, off=h)), R=[f"cumB{cb}", nbn], W=[f"Ldt{cb}_{hh}"])
                    T.op("dve", C("tensor_tensor", out=MTa[p].v(0, 128, g * 1024, [[128, 8], [1, 128]]), in0=Ldt[cb].v(0, 128, 0, [[128, 8], [1, 128]]),
                                  in1=PS.v(0, 128, psb(6, g * 128), [[0, 8], [1, 128]]), op=ALU.mult), R=[f"Ldt{cb}_{hh}" for hh in range(8)] + ["ps6"], W=[f"MT{p}_{g}"])

            def ssd_B(ti, tl):
                p = ti % 2
                dtn, dan, nbn, ecn, ten, den = [f"tv{p}_{k}" for k in range(6)]
                YB = [0, 1, 4, 5]
                if is_s:
                    T.dma("sp", full(state, 2048), sssm_d[tl["s"]], W=["state", "state0", "state1"])
                    T.op("act", C("activation", out=full(statebf, 2048), in_=full(state, 2048), func=AF.Copy), R=["state"], W=["statebf"])
                T.op("dve", C("tensor_tensor", out=xw.v(0, 128, 0, [[64, 32], [1, 64]]), in0=xc[ti].v(0, 128, 0, [[64, 32], [1, 64]]),
                              in1=tv[p].v(0, 128, 4 * 32, [[1, 32], [0, 64]]), op=ALU.mult), R=[f"xc{ti}", ten], W=["xw"])
                for g in range(4):
                    T.op("pe", C("matmul", PS.v(0, 128, psb(YB[g]), [[1, 512]]), CT.v(0, 128, g * 512 + ti * 128, [[1, 128]]), statebf.v(0, 128, g * 512, [[1, 512]]), start=True, stop=True),
                         R=["CT", "statebf"], W=[f"ps{YB[g]}"])
                for half in range(2):
                    b0 = YB[2 * half]
                    T.op("dve", C("tensor_tensor", out=yall.v(0, 128, half * 1024, [[64, 16], [1, 64]]), in0=PS.v(0, 128, psb(b0), [[64, 16], [1, 64]]),
                                  in1=tv[p].v(0, 128, 3 * 32 + half * 16, [[1, 16], [0, 64]]), op=ALU.mult), R=[f"ps{b0}", f"ps{b0 + 1}", ecn], W=[f"yall{half}"])
                for g in range(4):
                    T.op("pool", C("tensor_tensor", out=xD[0].v(0, 128, 0, [[64, 8], [1, 64]]), in0=xc[ti].v(0, 128, g * 512, [[64, 8], [1, 64]]),
                                   in1=dsk.v(0, 128, g * 8, [[1, 8], [0, 64]]), op=ALU.mult), R=[f"xc{ti}", "dsk"], W=["xD0"])
                    T.op("dve", C("tensor_tensor", out=yall.v(0, 128, g * 512, [[1, 512]]), in0=yall.v(0, 128, g * 512, [[1, 512]]), in1=full(xD[0], 512), op=ALU.add),
                         R=[f"yall{g // 2}", "xD0"], W=[f"yall{g // 2}"])
                for g in range(4):
                    for hh in range(8):
                        h = g * 8 + hh
                        T.op("pe", C("matmul", PS.v(0, 128, psb(YB[g], hh * 64), [[1, 64]]), MTa[p].v(0, 128, g * 1024 + hh * 128, [[1, 128]]), xc[ti].v(0, 128, h * 64, [[1, 64]]),
                                     start=True, stop=True), R=[f"MT{p}_{g}", f"xc{ti}"], W=[f"ps{YB[g]}"])
                for half in range(2):
                    b0 = YB[2 * half]
                    T.op("dve", C("tensor_tensor", out=yall.v(0, 128, half * 1024, [[1, 1024]]), in0=yall.v(0, 128, half * 1024, [[1, 1024]]), in1=PS.v(0, 128, psb(b0), [[1, 1024]]), op=ALU.add),
                         R=[f"yall{half}", f"ps{b0}", f"ps{b0 + 1}"], W=[f"yall{half}"])
                    T.op("pool", C("tensor_tensor", out=yall.v(0, 128, half * 1024, [[1, 1024]]), in0=yall.v(0, 128, half * 1024, [[1, 1024]]),
                                   in1=sz_bf[ti].v(0, 128, half * 1024, [[1, 1024]]), op=ALU.mult), R=[f"yall{half}", f"sz_bf{ti}"], W=[f"yall{half}"])
                ssn, ssf = sm()
                sidx = int(ssn[5:])
                for g in range(4):
                    T.op("act", C("activation", out=junk.v(0, 128, 0, [[1, 512]]), in_=yall.v(0, 128, g * 512, [[1, 512]]), func=AF.Square,
                                  accum_out=small.v(0, 128, sidx * 32 + g, [[1, 1]])), R=[f"yall{g // 2}"], W=["junk", ssn])
                rn, rf = rstd_from_ss(ssn, ssf(w=4), 512.0, width=4)
                ridx2 = int(rn[5:])
                T.op("dve", C("tensor_tensor", out=yzna.v(0, 128, 0, [[512, 4], [1, 512]]), in0=yall.v(0, 128, 0, [[512, 4], [1, 512]]),
                              in1=small.v(0, 128, ridx2 * 32, [[1, 4], [0, 512]]), op=ALU.mult), R=["yall0", "yall1", rn], W=["yzna"])
                for kt in range(16):
                    bank = 2 + kt // 8
                    T.op("pe", C("transpose", out=PSB.v(0, 128, 2 * psb(bank) + (kt % 8) * 128, [[1, 128]]), in_=yzna.v(0, 128, kt * 128, [[1, 128]]), identity=full(identb, 128)),
                         R=["yzna", "identb"], W=[f"ps{bank}"])
                for kt in range(16):
                    bank = 2 + kt // 8
                    T.op("act", C("activation", out=yzT.v(0, 128, kt * 512 + ti * 128, [[1, 128]]), in_=PSB.v(0, 128, 2 * psb(bank) + (kt % 8) * 128, [[1, 128]]), func=AF.Copy,
                                  scale=ssmw.v(0, 128, kt, [[1, 1]])), R=[f"ps{bank}", "ssmw"], W=[f"yzT{kt}"])
                for g in range(4):
                    T.op("pe", C("matmul", PS.v(0, 128, psb(YB[g]), [[1, 512]]), Btok[ti].v(0, 128, g * 128, [[1, 128]]), xw.v(0, 128, g * 512, [[1, 512]]), start=True, stop=True),
                         R=[f"Btok{ti}", "xw"], W=[f"ps{YB[g]}"])
                T.op("dve", C("tensor_tensor", out=state.v(0, 128, 0, [[64, 32], [1, 64]]), in0=state.v(0, 128, 0, [[64, 32], [1, 64]]),
                              in1=tv[p].v(0, 128, 5 * 32, [[1, 32], [0, 64]]), op=ALU.mult), R=["state0", "state1", "state", den], W=["state0", "state1"])
                for half in range(2):
                    b0 = YB[2 * half]
                    T.op("dve", C("tensor_tensor", out=state.v(0, 128, half * 1024, [[1, 1024]]), in0=state.v(0, 128, half * 1024, [[1, 1024]]), in1=PS.v(0, 128, psb(b0), [[1, 1024]]), op=ALU.add),
                         R=[f"state{half}", f"ps{b0}", f"ps{b0 + 1}"], W=[f"state{half}"])
                T.op("act", C("activation", out=full(statebf, 2048), in_=full(state, 2048), func=AF.Copy), R=["state0", "state1", "state"], W=["statebf"])
                if is_s:
                    T.dma("sp", ssms_o[tl["s"]], full(state, 2048), R=["state0", "state1"], is_out=True)
                elif tl["gi"] == NTP - 1:
                    T.dma("sp", ssmp_o, full(state, 2048), R=["state0", "state1"], is_out=True)

            recA = [T.rec(ssd_A, ti, tl) for ti, tl in enumerate(tiles)]
            recB = [T.rec(ssd_B, ti, tl) for ti, tl in enumerate(tiles)]
            T.play([recA[0]])
            for ti in range(NT):
                T.play([recB[ti]] + ([recA[ti + 1]] if ti + 1 < NT else []))

            stage(8)
            for c in range(2):
                s, nkt, nco = w_next("gs")
                for ct in range(4):
                    bank = ct % 2
                    formA(s, nkt, nco, ct, bank, hT, "hT")
                    T.op("act", C("activation", out=gsig.v(0, 128, ct * 512, [[1, NTOK]]), in_=PS.v(0, 128, psb(bank), [[1, NTOK]]), func=AF.Sigmoid),
                         R=[f"ps{bank}"], W=["gsig"])
                for c2 in range(2):
                    s, nkt, nco = w_next("so")
                    for ct in range(2):
                        bank = ct % 2
                        colt = c * 4 + c2 * 2 + ct
                        formA(s, nkt, nco, ct, bank, yzT, "yzT", per_kt=True)
                        T.op("dve", C("tensor_tensor", out=acc[0].v(0, 128, 0, [[1, NTOK]]), in0=PS.v(0, 128, psb(bank), [[1, NTOK]]),
                                                                                                 in1=gsig.v(0, 128, (c2 * 2 + ct) * 512, [[1, NTOK]]), op=ALU.mult), R=[f"ps{bank}", "gsig"], W=["acc0"])
                        T.op("pool", C("tensor_tensor", out=t1m.v(0, 128, colt * 512, [[1, NTOK]]), in0=acc[0].v(0, 128, 0, [[1, NTOK]]),
                                                                         in1=t1m.v(0, 128, colt * 512, [[1, NTOK]]), op=ALU.add), R=["acc0", "t1m"], W=["t1m"])
            stage(9)
            for c in range(2):
                s, nkt, nco = w_next("wo")
                for ti, tl in enumerate(tiles):
                    bank = ti % 2
                    for kt in range(8):
                        T.op("pe", C("matmul", PS.v(0, 128, psb(bank), [[1, 512]]), t1m.v(0, 128, kt * 512 + ti * 128, [[1, 128]]), wv(s, 8, 512, kt, 0, 512),
                                                                                  start=(kt == 0), stop=(kt == 7)), R=["t1m", f"wslot{s}"], W=[f"ps{bank}"])
                    T.op("act", C("activation", out=outf[ti].v(0, 128, c * 512, [[1, 512]]), in_=PS.v(0, 128, psb(bank), [[1, 512]]), func=AF.Copy),
                         R=[f"ps{bank}"], W=[f"outf{ti}"])

        onesb = sb("onesb", 2, BF16)
        dapad = sb("dapad", 128)
        T.op("dve", C("memset", full(dapad, 128), 0.0), W=["dapad"])
        T.op("dve", C("memset", full(onesb, 2), 1.0), W=["onesb"])

        try:
            stage(0)
            all_tiles = [[{"kind": "p", "gi": g0 * 4 + i} for i in range(4)] for g0 in range(NG)]
            if with_sample:
                all_tiles.append([{"kind": "s", "s": 0}, {"kind": "s", "s": 1}])
            T.play([T.rec(p0_emit, all_tiles[0])])
            for gidx, tl_ in enumerate(all_tiles):
                group(gidx, tl_)
                chains = [T.rec(post_emit, tl_)]
                if gidx + 1 < len(all_tiles):
                    chains.append(T.rec(p0_emit, all_tiles[gidx + 1]))
                T.play(chains)
        except _Stop:
            pass
        T.finish()

        block = es.enter_context(nc.Block())
        engmap = {"pe": block.tensor, "act": block.scalar, "dve": block.vector, "pool": block.gpsimd, "sp": block.sync}
        for en, deco in engmap.items():
            def body(e, en=en):
                for f in T.prog[en]:
                    f(e)
            deco(body)
    return nc, T


def host_tables(NTP):
    half = 8
    inv = (np.float32(500000.0) ** (-(np.arange(half, dtype=np.float32) * np.float32(2.0) / np.float32(16.0)))).astype(np.float32)
    cos2 = np.zeros((128, NTP + 1, 16), np.float32)
    sin2 = np.zeros((128, NTP + 1, 16), np.float32)
    for t in range(NTP + 1):
        pos = (np.arange(128, dtype=np.float32) + np.float32(128 * t)) if t < NTP else (np.float32(2048.0) + np.arange(128, dtype=np.float32))
        ang = (pos[:, None] * inv[None, :]).astype(np.float32)
        c, s_ = np.cos(ang).astype(np.float32), np.sin(ang).astype(np.float32)
        cos2[:, t, :8] = c
        cos2[:, t, 8:] = c
        sin2[:, t, :8] = s_
        sin2[:, t, 8:] = s_
    U = np.triu(np.ones((128, 128), np.float32))
    NEG = np.where(np.arange(128)[None, :] < np.arange(128)[:, None], np.float32(NEGV), np.float32(0.0)).astype(np.float32)
    ident = np.eye(128, dtype=np.float32)
    return cos2.reshape(128, -1), sin2.reshape(128, -1), U, NEG, ident


def make_in_maps(inp, NTP, n_cores):
    c2, s2, U, NEG, ident = host_tables(NTP)
    f = lambda a: np.ascontiguousarray(np.asarray(a, dtype=np.float32))
    shared = {
        "w_in": f(inp["w_in"][0]), "w_ao": f(inp["w_attn_o"][0]), "w_so": f(inp["w_ssm_o"][0]), "w_o": f(inp["w_out"][0]),
        "prew": f(inp["pre_norm_w"][0].reshape(8, 128).T), "ssmw": f(inp["ssm_norm_w"][0].reshape(16, 128).T),
        "postw": f(inp["post_norm_w"][0]),
        "convw": f(inp["conv_w"][0].reshape(4, 24, 128).transpose(2, 1, 0).reshape(128, 96)),
        "convb": f(inp["conv_b"][0].reshape(24, 128).T),
        "dtb": f(inp["dt_bias"][0]), "alog": f(inp["a_log"][0]), "dsk": f(inp["d_skip"][0]), "sinks": f(inp["sinks"][0]),
        "cos2": c2, "sin2": s2, "Umat": U, "NEGmat": NEG, "ident": ident,
    }
    maps = []
    for c in range(n_cores):
        m = dict(shared)
        m["x_p"] = f(inp["x_prompt"][c].reshape(-1, D)[:NTP * 128])
        sl = slice(2 * c, 2 * c + 2)
        m["x_s"] = f(inp["x_sample"][sl])
        ckk = np.asarray(inp["cache_k"][0, sl]).transpose(0, 2, 3, 1).reshape(2, 2, 2, 64, 128)
        m["ck"] = f(ckk.transpose(0, 2, 3, 1, 4).reshape(2, 128, 256))
        m["cv"] = f(inp["cache_v"][0, sl].reshape(2, 128, 256))
        m["sconv"] = f(inp["state_conv"][0, sl].reshape(2, 3, 24, 128).transpose(0, 3, 2, 1).reshape(2, 128, 72))
        m["sssm"] = f(inp["state_ssm"][0, sl].transpose(0, 3, 1, 2).reshape(2, 128, 2048))
        maps.append(m)
    return maps


_CACHE = {}


def run(inp, NTP, n_cores, trace=False):
    key = (NTP,)
    if key not in _CACHE:
        _CACHE[key] = build_program(NTP)[0]
    nc = _CACHE[key]
    maps = make_in_maps(inp, NTP, n_cores)
    res = run_bass_kernel_spmd(nc, maps, core_ids=list(range(n_cores)), trace=trace)
    R = res.results
    L = NTP * 128
    yp = np.stack([r["y_p"].reshape(L, D) for r in R])
    ys = np.concatenate([r["y_s"] for r in R], 0)
    kp = np.stack([r["kp_o"].reshape(128, 4, 64) for r in R])[None]
    vp = np.stack([r["vp_o"].reshape(128, 4, 64) for r in R])[None]
    cp = np.stack([r["convp_o"].reshape(128, 24, 3).transpose(2, 1, 0).reshape(3, 3072) for r in R])[None]
    sp = np.stack([r["ssmp_o"].reshape(128, 32, 64).transpose(1, 2, 0) for r in R])[None]
    ks = np.concatenate([r["ks_o"].reshape(2, 32, 4, 64) for r in R], 0)[None]
    vs = np.concatenate([r["vs_o"].reshape(2, 32, 4, 64) for r in R], 0)[None]
    cs = np.concatenate([r["convs_o"].reshape(2, 128, 24, 3).transpose(0, 3, 2, 1).reshape(2, 3, 3072) for r in R], 0)[None]
    ss = np.concatenate([r["ssms_o"].reshape(2, 128, 32, 64).transpose(0, 2, 3, 1) for r in R], 0)[None]
    outs = tuple(np.ascontiguousarray(a, dtype=np.float32) for a in (yp, ys, kp, vp, cp, sp, ks, vs, cs, ss))
    return outs, res


def kernel(**inputs):
    outs, _ = run(inputs, 64, 8)
    return outs
```

```python
import numpy as np
import concourse.bass as bass
import concourse.mybir as mybir
from concourse.bass_utils import run_bass_kernel_spmd

F32 = mybir.dt.float32
BF16 = mybir.dt.bfloat16
AF = mybir.ActivationFunctionType
ALU = mybir.AluOpType

D = 1024
INW = 9760
NEGV = -30000.0
import os as _os0
_os_env = _os0.environ.get
SAME_ENG_SYNC = bool(int(_os_env("K_SES", "1")))
import os as _os
STOP = float(_os.environ.get('K_STOP', '99'))


class _Stop(Exception):
    pass


def stage(k):
    if STOP <= k:
        raise _Stop()


def C(name, *a, **k):
    return (name, a, k)


class Buf:
    __slots__ = ("w", "r", "name")

    def __init__(self, name):
        self.name = name
        self.w = None
        self.r = {}


class Tracker:
    ENG = ("pe", "act", "dve", "pool", "sp")

    def __init__(self, nc, sems, dma_sems):
        self.nc = nc
        self.prog = {e: [] for e in self.ENG}
        self.cnt = {e: 0 for e in self.ENG}
        self.sem = sems
        self.semid = {e: ("E", e) for e in self.ENG}
        self.waited = {e: {} for e in self.ENG}
        self.dma_sems = dma_sems
        self.dma_cnt = [0] * len(dma_sems)
        self.dma_rr = 0
        self.dma_rr_sw = len(dma_sems) // 2
        self.bufs = {}
        self.out_events = []
        self.n = 0
        self.phase_of = {}
        self._rec = None
        self.phases = {}

    def set_phase(self, names, ph):
        self.phases.setdefault(ph, [])
        for n_ in names:
            self.phase_of[n_] = ph
            self.phases[ph].append(n_)

    def B(self, name):
        b = self.bufs.get(name)
        if b is None:
            b = self.bufs[name] = Buf(name)
        return b

    def _wait(self, eng, ev):
        key, val, sem, src = ev
        if src == eng and eng in ("pe",):
            return
        if src == eng and not SAME_ENG_SYNC and eng in ("act", "dve", "pool"):
            return
        if self.waited[eng].get(key, 0) >= val:
            return
        self.waited[eng][key] = val
        self.prog[eng].append(lambda e, s=sem, v=val: e.wait_ge(s, v))

    def _deps(self, eng, R, W):
        evs = []
        for b in R:
            b = self.B(b) if isinstance(b, str) else b
            if b.w is not None:
                evs.append(b.w)
            if b.name.startswith("ps"):
                evs.extend(ev for ev in b.r.values() if ev[3] != eng)
        for b in W:
            b = self.B(b) if isinstance(b, str) else b
            if b.w is not None:
                evs.append(b.w)
            evs.extend(b.r.values())
            ph = self.phase_of.get(b.name)
            if ph is not None:
                for oph, names in self.phases.items():
                    if oph == ph or oph[:2] != ph[:2]:
                        continue
                    for n_ in names:
                        ob = self.bufs.get(n_)
                        if ob is None:
                            continue
                        if ob.w is not None:
                            evs.append(ob.w)
                        evs.extend(ob.r.values())
                        ob.w = None
                        ob.r = {}
        for ev in evs:
            self._wait(eng, ev)

    def _mark(self, ev, R, W):
        for b in R:
            b = self.B(b) if isinstance(b, str) else b
            old = b.r.get(ev[0])
            if old is None or old[1] < ev[1]:
                b.r[ev[0]] = ev
        for b in W:
            b = self.B(b) if isinstance(b, str) else b
            b.w = ev
            b.r = {}

    def rec(self, fn, *a):
        saved = self._rec
        self._rec = lst = []
        try:
            fn(*a)
        finally:
            self._rec = saved
        return lst

    def play(self, lists):
        lists = [l for l in lists if l]
        pos = [0] * len(lists)
        while True:
            best, bi = None, -1
            for i, l in enumerate(lists):
                if pos[i] < len(l):
                    fr = pos[i] / len(l)
                    if best is None or fr < best:
                        best, bi = fr, i
            if bi < 0:
                break
            item = lists[bi][pos[bi]]
            pos[bi] += 1
            if item[0] == "op":
                self.op(*item[1:])
            else:
                self.dma(item[1], item[2], item[3], R=item[4], W=item[5], is_out=item[6], **item[7])

    def op(self, eng, fn, R=(), W=()):
        if self._rec is not None:
            self._rec.append(("op", eng, fn, tuple(R), tuple(W)))
            return None
        self.n += 1
        self._deps(eng, R, W)
        self.cnt[eng] += 1
        sem = self.sem[eng]
        ev = (self.semid[eng], self.cnt[eng], sem, eng)
        nm_, a_, k_ = fn
        self.prog[eng].append(lambda e, nm_=nm_, a_=a_, k_=k_, s=sem: getattr(e, nm_)(*a_, **k_).then_inc(s, 1))
        self._mark(ev, R, W)
        return ev

    def dma(self, q, out, in_, R=(), W=(), is_out=False, **kw):
        if self._rec is not None:
            self._rec.append(("dma", q, out, in_, tuple(R), tuple(W), is_out, kw))
            return None
        self.n += 1
        self._deps(q, R, W)
        half = len(self.dma_sems) // 2
        if q == "pool":
            i = self.dma_rr_sw
            self.dma_rr_sw = half + (self.dma_rr_sw + 1 - half) % (len(self.dma_sems) - half)
        else:
            i = self.dma_rr
            self.dma_rr = (self.dma_rr + 1) % half
        sem = self.dma_sems[i]
        prev = self.dma_cnt[i]
        if prev > 0:
            self._wait(q, (("D", i), prev, sem, None))
        self.dma_cnt[i] = prev + 16
        ev = (("D", i), prev + 16, sem, None)
        self.prog[q].append(lambda e, o=out, a=in_, s=sem, k=kw: e.dma_start(out=o, in_=a, **k).then_inc(s, 16))
        self._mark(ev, R, W)
        if is_out:
            self.out_events.append(ev)
        return ev

    def finish(self):
        for ev in self.out_events:
            self._wait("sp", ev)
        for i, c in enumerate(self.dma_cnt):
            if c > 0:
                self._wait("sp", (("D", i), c, self.dma_sems[i], None))
        for e in ("pe", "act", "dve", "pool"):
            if self.cnt[e] > 0:
                self._wait("sp", (self.semid[e], self.cnt[e], self.sem[e], e))


class TT:
    def __init__(self, h):
        self.h = h
        a = h[:] if not isinstance(h, bass.AP) else h
        self.t = a.tensor
        self.ps = a.ap[0][0]
        self.base = a.offset

    def v(self, p0, p1, off, pat):
        return bass.AP(self.t, self.base + p0 * self.ps + off, [[self.ps, p1 - p0]] + [list(x) for x in pat])


def build_program(NTP, with_sample=True):
    NG = NTP // 4
    NTAB = NTP + 1
    nc = bass.Bass("TRN2", target_bir_lowering=False)

    def din(name, shape):
        return nc.dram_tensor(name, list(shape), F32, kind="ExternalInput").ap()

    def dout(name, shape):
        return nc.dram_tensor(name, list(shape), F32, kind="ExternalOutput").ap()

    x_p = din("x_p", [NTP * 128, D])
    x_s = din("x_s", [2, 32, D])
    ck_d = din("ck", [2, 128, 256])
    cv_d = din("cv", [2, 128, 256])
    sconv_d = din("sconv", [2, 128, 72])
    sssm_d = din("sssm", [2, 128, 2048])
    w_in = din("w_in", [D, INW])
    w_ao = din("w_ao", [1024, 1024])
    w_so = din("w_so", [2048, 1024])
    w_o = din("w_o", [1024, 1024])
    prew_d = din("prew", [128, 8])
    ssmw_d = din("ssmw", [128, 16])
    postw_d = din("postw", [1024])
    convw_d = din("convw", [128, 96])
    convb_d = din("convb", [128, 24])
    dtb_d = din("dtb", [32])
    alog_d = din("alog", [32])
    dsk_d = din("dsk", [32])
    sinks_d = din("sinks", [16])
    cos_d = din("cos2", [128, NTAB * 16])
    sin_d = din("sin2", [128, NTAB * 16])
    U_d = din("Umat", [128, 128])
    NEG_d = din("NEGmat", [128, 128])
    id_d = din("ident", [128, 128])

    y_p = dout("y_p", [NTP * 128, D])
    y_s = dout("y_s", [2, 32, D])
    kp_o = dout("kp_o", [128, 256])
    vp_o = dout("vp_o", [128, 256])
    convp_o = dout("convp_o", [128, 72])
    ssmp_o = dout("ssmp_o", [128, 2048])
    ks_o = dout("ks_o", [2, 32, 256])
    vs_o = dout("vs_o", [2, 32, 256])
    convs_o = dout("convs_o", [2, 128, 72])
    ssms_o = dout("ssms_o", [2, 128, 2048])
    cumscr = nc.dram_tensor("cumscr", [4, 32, 128], F32, kind="Internal").ap()

    import contextlib
    es = contextlib.ExitStack()
    with es:
        def sb(name, free, dt=F32):
            return TT(es.enter_context(nc.sbuf_tensor(name, [128, free], dt)))

        identb = sb("identb", 128, BF16)
        identf = sb("identf", 128)
        Umat = sb("Umat_s", 128)
        ones = sb("ones_s", 128)
        NEGm = sb("NEG_s", 128)
        prew = sb("prew_s", 8)
        ssmw = sb("ssmw_s", 16)
        postw = sb("postw_s", 1024)
        convw = sb("convw_s", 96)
        convb = sb("convb_s", 24)
        dtb = sb("dtb_s", 32)
        a_b = sb("a_s", 32)
        dsk = sb("dsk_s", 32)
        esink = sb("esink_s", 16)
        NSLOT = 3
        wslot = [sb(f"wslot{i}", 4096, BF16) for i in range(NSLOT)]
        xin = [sb(f"xin{i}", 1024) for i in range(2)]
        junk = sb("junk", 1024, BF16)
        hbf = sb("hbf", 1024, BF16)
        hT = sb("hT", 8 * 512, BF16)
        t1m = sb("t1m", 8 * 512, BF16)
        state = sb("state", 2048)
        statebf = sb("statebf", 2048, BF16)
        Vext = [sb(f"vext{i}", 4 * 65, BF16) for i in range(5)]
        Vc = [sb(f"vc{i}", 4 * 65, BF16) for i in range(2)]
        kTr = [sb(f"kT{i}", 512, BF16) for i in range(2)]
        kTc = [sb(f"kTc{i}", 512, BF16) for i in range(2)]
        small = sb("small", 32 * 32)
        arena_h = es.enter_context(nc.sbuf_tensor("arena", [128, 12288], F32))
        arena_f = arena_h[:]
        arena_b = arena_h.bitcast(BF16)[:]
        _ar = {"p1": 0, "p2": 0, "p3": 0}

        def ar(ph, nbytes, dt):
            o = _ar[ph]
            _ar[ph] = o + nbytes
            assert _ar[ph] <= 49152, (ph, _ar[ph])
            if dt == F32:
                return TT(arena_f[:, o // 4:(o + nbytes) // 4])
            return TT(arena_b[:, o // 2:(o + nbytes) // 2])

        q_bf = [ar("p1", 2048, BF16) for i in range(4)]
        sg_bf = [ar("p1", 2048, BF16) for i in range(4)]
        qT = ar("p1", 4096, BF16)
        PT = [[ar("p1", 1024, BF16) for i in range(2)] for a in range(2)]
        attn_f = ar("p1", 4096, F32)
        attg = ar("p1", 2048, BF16)
        attgT = ar("p1", 8192, BF16)
        k_bf = [ar("p1", 512, BF16) for i in range(4)]
        kf = ar("p1", 1024, F32)
        vf = ar("p1", 1024, F32)
        ropeA = ar("p1", 1024, F32)
        ropeB = ar("p1", 1024, F32)
        sz_bf = [ar("p2", 4096, BF16) for i in range(4)]
        xc = [ar("p2", 4096, BF16) for i in range(4)]
        yzT = ar("p2", 16384, BF16)
        outf = [ar("p3", 4096, F32) for i in range(4)]
        ytmp = ar("p3", 4096, F32)
        xr = [TT(arena_f[:, 24576 // 4:(24576 + 4096) // 4]), TT(arena_f[:, 30720 // 4:(30720 + 4096) // 4])]
        pjunk = TT(arena_b[:, 34816 // 2:(34816 + 2048) // 2])
        gsig = sb("gsig", 4 * 512, BF16)
        csg = sb("csg", 128)
        snn = sb("snn", 128)
        arena2_h = es.enter_context(nc.sbuf_tensor("arena2", [128, 4096], F32))
        arena2_f = arena2_h[:]
        arena2_b = arena2_h.bitcast(BF16)[:]
        _ar2 = {"c": 0, "s": 0}

        def ar2(ph, nbytes, dt):
            o = _ar2[ph]
            _ar2[ph] = o + nbytes
            assert _ar2[ph] <= 16384, (ph, _ar2[ph])
            if dt == F32:
                return TT(arena2_f[:, o // 4:(o + nbytes) // 4])
            return TT(arena2_b[:, o // 2:(o + nbytes) // 2])

        raw_sb = [ar2("c", 2080, F32) for i in range(2)]
        acc = [ar2("c", 2048, F32) for i in range(2)]
        xcfm = [ar2("c", 1024, BF16) for i in range(4)]
        MTa = [ar2("s", 8192, BF16) for i in range(2)]
        tv = [sb(f"tv{i}", 8 * 32) for i in range(2)]
        BT = sb("BT", 4 * 512, BF16)
        CT = sb("CT", 4 * 512, BF16)
        Btok = [sb(f"Btok{i}", 512, BF16) for i in range(4)]
        dtraw = [sb(f"dtraw{i}", 32) for i in range(4)]
        tailb = sb("tailb", 72)
        convo = [sb(f"convo{i}", 72) for i in range(2)]
        sconv = [sb(f"sconv{i}", 72) for i in range(2)]
        cumTs = sb("cumTs", 128)
        cumB = [sb(f"cumB{i}", 1024) for i in range(2)]
        Ldt = [sb(f"Ldt{i}", 1024, BF16) for i in range(2)]
        yall = sb("yall", 2048)
        xD = [sb("xD0", 512)]
        yzna = sb("yzna", 2048, BF16)
        xw = sb("xw", 2048, BF16)

        psum = es.enter_context(nc.psum_tensor("psum", [128, 8 * 512], F32))
        PS = TT(psum)
        PSB = TT(psum.bitcast(BF16))

        sems = {e: es.enter_context(nc.semaphore(f"sem_{e}")) for e in Tracker.ENG}
        dma_sems = [es.enter_context(nc.semaphore(f"dsem{i}")) for i in range(48)]
        T = Tracker(nc, sems, dma_sems)
        T.set_phase([f"q_bf{i}" for i in range(4)] + [f"sg_bf{i}" for i in range(4)] + ["qT", "PT00", "PT01", "PT10", "PT11", "attn_f", "attg", "attgT"]
                    + [f"k_bf{i}" for i in range(4)] + ["kf", "vf", "ropeA", "ropeB"], "A1p1")
        T.set_phase([f"sz_bf{i}" for i in range(4)] + [f"xc{i}" for i in range(4)] + [f"yzT{k}" for k in range(16)], "A1p2")
        T.set_phase([f"outf{i}" for i in range(4)] + ["ytmp", "xr0", "xr1", "pjunk"], "A1p3")
        T.set_phase(["raw0", "raw1", "acc0", "acc1", "xcfm0", "xcfm1", "xcfm2", "xcfm3"], "A2c")
        T.set_phase([f"MT{p}_{g}" for p in range(2) for g in range(4)], "A2s")

        def psb(b, off=0):
            return b * 512 + off

        def full(t, n, off=0, p0=0, p1=128):
            return t.v(p0, p1, off, [[1, n]])

        _sm = [0]

        def sm(n=32):
            i = _sm[0]
            _sm[0] = (i + 1) % 32
            return ("small%d" % i, lambda p0=0, p1=128, w=n, ii=i: small.v(p0, p1, ii * 32, [[1, w]]))

        def ld(q, t, n, src, name, **kw):
            T.dma(q, full(t, n), src, W=[name], **kw)

        ld("sp", identf, 128, id_d, "identf")
        ld("sp", Umat, 128, U_d, "Umat")
        ld("sp", NEGm, 128, NEG_d, "NEG")
        ld("sp", prew, 8, prew_d, "prew")
        ld("sp", ssmw, 16, ssmw_d, "ssmw")
        ld("sp", postw, 1024, postw_d.partition_broadcast(128), "postw")
        ld("sp", convw, 96, convw_d, "convw")
        ld("sp", convb, 24, convb_d, "convb")
        ld("sp", dtb, 32, dtb_d.partition_broadcast(128), "dtb")
        ld("sp", a_b, 32, alog_d.partition_broadcast(128), "a_b")
        ld("sp", dsk, 32, dsk_d.partition_broadcast(128), "dsk")
        ld("sp", esink, 16, sinks_d.partition_broadcast(128), "esink")
        T.op("dve", C("tensor_copy", out=full(identb, 128), in_=full(identf, 128)), R=["identf"], W=["identb"])
        T.op("dve", C("memset", full(ones, 128), 1.0), W=["ones"])
        T.op("act", C("activation", out=full(a_b, 32), in_=full(a_b, 32), func=AF.Exp), R=["a_b"], W=["a_b"])
        T.op("dve", C("tensor_scalar", out=full(a_b, 32), in0=full(a_b, 32), scalar1=-1.0, scalar2=None, op0=ALU.mult), R=["a_b"], W=["a_b"])
        T.op("act", C("activation", out=full(esink, 16), in_=full(esink, 16), func=AF.Exp), R=["esink"], W=["esink"])
        T.op("dve", C("memset", full(state, 2048), 0.0), W=["state", "state0", "state1"])
        T.op("dve", C("memset", full(statebf, 2048), 0.0), W=["statebf"])
        T.op("dve", C("memset", full(tailb, 72), 0.0), W=["tailb"])
        for i in range(5):
            T.op("pool", C("memset", full(Vext[i], 260), 1.0), W=[f"vext{i}"])
        for i in range(2):
            T.op("pool", C("memset", full(Vc[i], 260), 1.0), W=[f"vc{i}"])
            for a in range(2):
                T.op("pool", C("memset", full(PT[a][i], 512), 0.0), W=[f"PT{a}{i}"])
            T.op("pool", C("memset", full(xin[i], 1024), 0.0), W=[f"xin{i}"])

        def w_in_chunk(c0, nco):
            return (w_in.rearrange("(kt p) c -> p kt c", p=128)[:, :, c0:c0 + nco], 8, nco)

        def w_chunk(w, nkt, c0, nco):
            return (w.rearrange("(kt p) c -> p kt c", p=128)[:, :, c0:c0 + nco], nkt, nco)

        chunk_list = []
        chunk_list += [("q", w_in_chunk(0, 512)), ("q", w_in_chunk(512, 512)), ("kv", w_in_chunk(1024, 512)),
                       ("g", w_in_chunk(1536, 512)), ("g", w_in_chunk(2048, 512))]
        chunk_list += [("ga", w_in_chunk(7712, 512)), ("ao", w_chunk(w_ao, 8, 0, 512)),
                       ("ga", w_in_chunk(8224, 512)), ("ao", w_chunk(w_ao, 8, 512, 512))]
        chunk_list += [("z", w_in_chunk(2560 + 512 * i, 512)) for i in range(4)]
        chunk_list += [("xbc", w_in_chunk(4608 + 512 * i, 512)) for i in range(6)]
        chunk_list += [("dt", w_in_chunk(7680, 32))]
        for i in range(2):
            chunk_list += [("gs", w_in_chunk(8736 + 512 * i, 512)),
                           ("so", w_chunk(w_so, 16, 512 * i, 256)), ("so", w_chunk(w_so, 16, 512 * i + 256, 256))]
        chunk_list += [("wo", w_chunk(w_o, 8, 0, 512)), ("wo", w_chunk(w_o, 8, 512, 512))]
        NCH = len(chunk_list)
        total_groups = NG + (1 if with_sample else 0)
        wstate = {"loaded": 0, "used": 0}

        def w_prefetch(upto):
            while wstate["loaded"] <= upto and wstate["loaded"] < NCH * total_groups:
                gidx = wstate["loaded"]
                kind, (src, nkt, nco) = chunk_list[gidx % NCH]
                s = gidx % NSLOT
                dst = wslot[s].v(0, 128, 0, [[nco, nkt], [1, nco]])
                T.dma("pool", dst, src, W=[f"wslot{s}"])
                wstate["loaded"] += 1

        def w_next(expect):
            gidx = wstate["used"]
            kind, (src, nkt, nco) = chunk_list[gidx % NCH]
            assert kind == expect, (kind, expect)
            w_prefetch(gidx + NSLOT - 1)
            wstate["used"] += 1
            s = gidx % NSLOT
            return s, nkt, nco

        def wv(s, nkt, nco, kt, c0, n):
            return wslot[s].v(0, 128, kt * nco + c0, [[1, n]])

        def rstd_from_ss(ss_name, ss_ap, n, width=1):
            nm, f = sm()
            T.op("dve", C("tensor_scalar", out=f(w=width), in0=ss_ap, scalar1=1.0 / n, scalar2=1e-6, op0=ALU.mult, op1=ALU.add), R=[ss_name], W=[nm])
            T.op("act", C("activation", out=f(w=width), in_=f(w=width), func=AF.Ln), R=[nm], W=[nm])
            T.op("act", C("activation", out=f(w=width), in_=f(w=width), func=AF.Exp, scale=-0.5), R=[nm], W=[nm])
            return nm, f

        def p0_emit(tiles):
            is_s = tiles[0]["kind"] == "s"
            for ti, tl in enumerate(tiles):
                xb = ti % 2
                if is_s:
                    T.dma("sp", full(xin[xb], 1024, p1=32), x_s[tl["s"]], W=[f"xin{xb}"])
                else:
                    gi = tl["gi"]
                    T.dma("sp", full(xin[xb], 1024), x_p[gi * 128:(gi + 1) * 128, :], W=[f"xin{xb}"])
                ssn, ssf = sm()
                T.op("act", C("activation", out=full(junk, 1024), in_=full(xin[xb], 1024), func=AF.Square, accum_out=ssf(w=1)),
                     R=[f"xin{xb}"], W=["junk", ssn])
                rn, rf = rstd_from_ss(ssn, ssf(w=1), 1024.0)
                T.op("dve", C("tensor_scalar", out=full(hbf, 1024), in0=full(xin[xb], 1024), scalar1=rf(w=1), scalar2=None, op0=ALU.mult),
                     R=[f"xin{xb}", rn], W=["hbf"])
                for kt in range(8):
                    T.op("pe", C("transpose", out=PSB.v(0, 128, 2 * psb(2) + kt * 128, [[1, 128]]), in_=hbf.v(0, 128, kt * 128, [[1, 128]]), identity=full(identb, 128)),
                         R=["hbf", "identb"], W=["ps2"])
                for kt in range(8):
                    T.op("act", C("activation", out=hT.v(0, 128, kt * 512 + ti * 128, [[1, 128]]), in_=PSB.v(0, 128, 2 * psb(2) + kt * 128, [[1, 128]]),
                                                                      func=AF.Copy, scale=prew.v(0, 128, kt, [[1, 1]])),
                         R=["ps2", "prew"], W=["hT"])

        def post_emit(tiles):
            is_s = tiles[0]["kind"] == "s"
            for ti, tl in enumerate(tiles):
                xb = ti % 2
                if is_s:
                    T.dma("sp", full(xr[xb], 1024, p1=32), x_s[tl["s"]], W=[f"xr{xb}"])
                else:
                    gi = tl["gi"]
                    T.dma("sp", full(xr[xb], 1024), x_p[gi * 128:(gi + 1) * 128, :], W=[f"xr{xb}"])
                ssn, ssf = sm()
                T.op("act", C("activation", out=full(pjunk, 1024), in_=full(outf[ti], 1024), func=AF.Square, accum_out=ssf(w=1)), R=[f"outf{ti}"], W=["pjunk", ssn])
                rn, rf = rstd_from_ss(ssn, ssf(w=1), 1024.0)
                T.op("dve", C("scalar_tensor_tensor", out=full(ytmp, 1024), in0=full(outf[ti], 1024), scalar=rf(w=1), in1=full(postw, 1024), op0=ALU.mult, op1=ALU.mult),
                     R=[f"outf{ti}", rn, "postw"], W=["ytmp"])
                T.op("dve", C("tensor_tensor", out=full(ytmp, 1024), in0=full(ytmp, 1024), in1=full(xr[xb], 1024), op=ALU.add), R=["ytmp", f"xr{xb}"], W=["ytmp"])
                if is_s:
                    T.dma("sp", y_s[tl["s"]], full(ytmp, 1024, p1=32), R=["ytmp"], is_out=True)
                else:
                    gi = tl["gi"]
                    T.dma("sp", y_p[gi * 128:(gi + 1) * 128, :], full(ytmp, 1024), R=["ytmp"], is_out=True)

        def group(g0, tiles):
            NT = len(tiles)
            NTOK = NT * 128
            is_s = tiles[0]["kind"] == "s"

            if is_s and False:
                pass

            stage(1)
            def formB(s, nkt, nco, ti, bank, n=None, c0=0):
                n = nco if n is None else n
                for kt in range(nkt):
                    T.op("pe", C("matmul", PS.v(0, 128, psb(bank), [[1, n]]), hT.v(0, 128, kt * 512 + ti * 128, [[1, 128]]), wv(s, nkt, nco, kt, c0, n),
                                                         start=(kt == 0), stop=(kt == nkt - 1)),
                         R=["hT", f"wslot{s}"], W=[f"ps{bank}"])

            def formA(s, nkt, nco, ct, bank, rhsT, rname, rstride=512, per_kt=False):
                for kt in range(nkt):
                    rn_ = f"{rname}{kt}" if per_kt else rname
                    T.op("pe", C("matmul", PS.v(0, 128, psb(bank), [[1, NTOK]]), wv(s, nkt, nco, kt, ct * 128, 128), rhsT.v(0, 128, kt * rstride, [[1, NTOK]]),
                                                         start=(kt == 0), stop=(kt == nkt - 1)),
                         R=[rn_, f"wslot{s}"], W=[f"ps{bank}"])

            if is_s:
                T.dma("sp", csg.v(0, 128, 0, [[1, 16]]), cos_d[:, NTP * 16:NTP * 16 + 16], W=["csg"])
                T.dma("sp", snn.v(0, 128, 0, [[1, 16]]), sin_d[:, NTP * 16:NTP * 16 + 16], W=["snn"])
            else:
                T.dma("sp", csg.v(0, 128, 0, [[1, 64]]), cos_d[:, g0 * 64:g0 * 64 + 64], W=["csg"])
                T.dma("sp", snn.v(0, 128, 0, [[1, 64]]), sin_d[:, g0 * 64:g0 * 64 + 64], W=["snn"])

            def tabidx(tl):
                return 0 if tl["kind"] == "s" else tl["gi"] % 4

            def rope(src_bank, nheads, tl, dst, dname, dst_hstride, perm=False):
                ti_ = tabidx(tl)
                cs = csg.v(0, 128, ti_ * 16, [[0, nheads], [1, 16]])
                sn = snn.v(0, 128, ti_ * 16, [[0, nheads], [1, 16]])
                src = PS.v(0, 128, psb(src_bank), [[64, nheads], [1, 16]])
                T.op("dve", C("tensor_tensor", out=ropeA.v(0, 128, 0, [[16, nheads], [1, 16]]), in0=src, in1=cs, op=ALU.mult), R=[f"ps{src_bank}", "csg"], W=["ropeA"])
                stage(1.45)
                T.op("dve", C("tensor_tensor", out=ropeB.v(0, 128, 0, [[16, nheads], [1, 16]]), in0=src, in1=sn, op=ALU.mult), R=[f"ps{src_bank}", "snn"], W=["ropeB"])
                stage(1.5)
                if perm:
                    op_ = [[64, 2], [128, 4], [1, 8]]
                    ip_ = [[64, 2], [16, 4], [1, 8]]
                else:
                    op_ = [[dst_hstride, nheads], [1, 8]]
                    ip_ = [[16, nheads], [1, 8]]
                T.op("dve", C("tensor_tensor", out=dst.v(0, 128, 0, op_), in0=ropeA.v(0, 128, 0, ip_),
                                                      in1=ropeB.v(0, 128, 8, ip_), op=ALU.subtract), R=["ropeA", "ropeB"], W=[dname])
                stage(1.55)
                T.op("dve", C("tensor_tensor", out=dst.v(0, 128, 8, op_), in0=ropeA.v(0, 128, 8, ip_),
                                                      in1=ropeB.v(0, 128, 0, ip_), op=ALU.add), R=["ropeA", "ropeB"], W=[dname])

            for c in range(2):
                s, nkt, nco = w_next("q")
                for ti, tl in enumerate(tiles):
                    bank = ti % 2
                    stage(1.1)
                    formB(s, nkt, nco, ti, bank)
                    stage(1.2)
                    T.op("act", C("activation", out=q_bf[ti].v(0, 128, c * 512, [[64, 2], [128, 4], [1, 64]]), in_=PS.v(0, 128, psb(bank), [[256, 2], [64, 4], [1, 64]]), func=AF.Copy),
                         R=[f"ps{bank}"], W=[f"q_bf{ti}"])
                    stage(1.4)
                    rope(bank, 8, tl, TT(q_bf[ti].v(0, 128, c * 512, [[1, 512]])), f"q_bf{ti}", 64, perm=True)
                    stage(1.6)
            s, nkt, nco = w_next("kv")
            for ti, tl in enumerate(tiles):
                bank = ti % 2
                vslot = (g0 * 4 + ti) % 5 if not is_s else None
                formB(s, nkt, nco, ti, bank)
                T.op("act", C("activation", out=full(kf, 256), in_=PS.v(0, 128, psb(bank), [[1, 256]]), func=AF.Copy), R=[f"ps{bank}"], W=["kf"])
                rope(bank, 4, tl, kf, "kf", 64)
                T.op("act", C("activation", out=full(k_bf[ti], 256), in_=full(kf, 256), func=AF.Copy), R=["kf"], W=[f"k_bf{ti}"])
                if is_s:
                    vt, vn = Vext[ti], f"vext{ti}"
                else:
                    vt, vn = Vext[vslot], f"vext{vslot}"
                tl["vt"], tl["vn"] = vt, vn
                T.op("dve", C("tensor_copy", out=vt.v(0, 128, 0, [[65, 4], [1, 64]]), in_=PS.v(0, 128, psb(bank, 256), [[64, 4], [1, 64]])),
                     R=[f"ps{bank}"], W=[vn])
                want_out = is_s or (tl["gi"] == NTP - 1)
                if want_out:
                    T.op("act", C("activation", out=full(vf, 256), in_=PS.v(0, 128, psb(bank, 256), [[1, 256]]), func=AF.Copy), R=[f"ps{bank}"], W=["vf"])
                    if is_s:
                        T.dma("sp", ks_o[tl["s"]], full(kf, 256, p1=32), R=["kf"], is_out=True)
                        T.dma("sp", vs_o[tl["s"]], full(vf, 256, p1=32), R=["vf"], is_out=True)
                    else:
                        T.dma("sp", kp_o, full(kf, 256), R=["kf"], is_out=True)
                        T.dma("sp", vp_o, full(vf, 256), R=["vf"], is_out=True)
            for c in range(2):
                s, nkt, nco = w_next("g")
                for ti, tl in enumerate(tiles):
                    bank = ti % 2
                    formB(s, nkt, nco, ti, bank)
                    T.op("act", C("activation", out=sg_bf[ti].v(0, 128, c * 512, [[1, 512]]), in_=PS.v(0, 128, psb(bank), [[1, 512]]), func=AF.Silu),
                         R=[f"ps{bank}"], W=[f"sg_bf{ti}"])

            stage(2)
            if not is_s:
                for i in range(2):
                    T.op("pool", C("memset", PT[0][i].v(0, 64, 64, [[128, 4], [1, 64]]), 0.0), W=[f"PT0{i}"])
                    T.op("pool", C("memset", PT[1][i].v(64, 128, 0, [[128, 4], [1, 64]]), 0.0), W=[f"PT1{i}"])
            for ti, tl in enumerate(tiles):
                for t in range(8):
                    T.op("pe", C("transpose", out=PSB.v(0, 128, 2 * psb(2) + t * 128, [[1, 128]]),
                                 in_=q_bf[ti].v(0, 128, t * 128, [[1, 128]]), identity=full(identb, 128)),
                         R=[f"q_bf{ti}", "identb"], W=["ps2"])
                T.op("act", C("activation", out=qT.v(0, 128, 0, [[1, 1024]]), in_=PSB.v(0, 128, 2 * psb(2), [[1, 1024]]), func=AF.Copy),
                     R=["ps2"], W=["qT"])
                stage(2.2)
                if is_s:
                    kcur, kcn = kTr[0], "kT0"
                    kprev, kpn = kTc[ti], f"kTc{ti}"
                    vprev, vpn = Vc[ti], f"vc{ti}"
                    T.dma("pool", full(kTc[ti], 256), ck_d[tl["s"]], W=[kpn])
                    T.dma("pool", Vc[ti].v(0, 128, 0, [[65, 4], [1, 64]]), cv_d[tl["s"]].rearrange("p (h d) -> p h d", h=4), W=[vpn])
                    hasA = True
                    nkB = 32
                else:
                    gi = tl["gi"]
                    kcur, kcn = kTr[gi % 2], f"kT{gi % 2}"
                    kprev, kpn = kTr[(gi + 1) % 2], f"kT{(gi + 1) % 2}"
                    pv = (gi - 1) % 5
                    vprev, vpn = Vext[pv], f"vext{pv}"
                    hasA = gi > 0
                    nkB = 128
                for u in range(2):
                    T.op("pe", C("transpose", out=PSB.v(0, 128, 2 * psb(6) + u * 128, [[1, 128]]), in_=k_bf[ti].v(0, 128, u * 128, [[1, 128]]), identity=full(identb, 128)),
                         R=[f"k_bf{ti}", "identb"], W=["ps6"])
                T.op("act", C("activation", out=full(kcur, 256), in_=PSB.v(0, 128, 2 * psb(6), [[1, 256]]), func=AF.Copy), R=["ps6"], W=[kcn])
                stage(2.4)
                vcur, vcn = tl["vt"], tl["vn"]
                def att_chain(hk):
                    bA = 2 if hk % 2 == 0 else 0
                    bB = 3 if hk % 2 == 0 else 1
                    pa = PT[0][hk % 2]
                    pan = f"PT0{hk % 2}"
                    pb_ = PT[1][hk % 2]
                    pbn = f"PT1{hk % 2}"
                    kp0 = (hk % 2) * 64
                    ku = hk // 2
                    qrhs = qT.v(kp0, kp0 + 64, (0 if hk < 2 else 4) * 128, [[1, 512]])
                    if hasA:
                        T.op("pe", C("matmul", PS.v(0, 128, psb(bA), [[1, 512]]), kprev.v(kp0, kp0 + 64, ku * 128, [[1, 128]]), qrhs, start=True, stop=True),
                             R=[kpn, "qT"], W=[f"ps{bA}"])
                        if is_s:
                            regs = [(0, 128, 0, 128)]
                        else:
                            regs = [(0, 128, 0, 64), (64, 128, 64, 64)]
                        for (p0, p1, q0, nq) in regs:
                            T.op("act", C("activation", out=pa.v(p0, p1, q0, [[128, 4], [1, nq]]), in_=PS.v(p0, p1, psb(bA, q0), [[128, 4], [1, nq]]),
                                                                                                 func=AF.Exp, scale=0.125), R=[f"ps{bA}"], W=[pan])
                    T.op("pe", C("matmul", PS.v(0, 128, psb(bB), [[1, 512]]), kcur.v(kp0, kp0 + 64, ku * 128, [[1, 128]]), qrhs, start=True, stop=True),
                         R=[kcn, "qT"], W=[f"ps{bB}"])
                    if is_s:
                        regs = [(0, 32, 0, 128)]
                    else:
                        regs = [(0, 64, 0, 128), (64, 128, 64, 64)]
                    for (p0, p1, q0, nq) in regs:
                        T.op("act", C("activation", out=pb_.v(p0, p1, q0, [[128, 4], [1, nq]]), in_=PS.v(p0, p1, psb(bB, q0), [[128, 4], [1, nq]]),
                                                                                               func=AF.Exp, scale=0.125), R=[f"ps{bB}"], W=[pbn])
                    for r in range(4):
                        h = hk * 4 + r
                        obank = 4 + hk % 2
                        ooff = psb(obank, ((hk // 2) * 4 + r) * 64)
                        sbank = 7 if hk % 2 == 0 else 6
                        soff = psb(sbank, h * 2)
                        if hasA:
                            T.op("pe", C("matmul", PS.v(0, 128, ooff, [[1, 64]]), pa.v(0, 128, r * 128, [[1, 128]]), vprev.v(0, 128, hk * 65, [[1, 64]]),
                                                                                                    start=True, stop=False), R=[pan, vpn], W=[f"ps{obank}"])
                            T.op("pe", C("matmul", PS.v(0, 128, soff, [[1, 2]]), pa.v(0, 128, r * 128, [[1, 128]]), onesb.v(0, 128, 0, [[1, 2]]),
                                                                                                    start=True, stop=False), R=[pan, "onesb"], W=[f"ps{sbank}"])
                        T.op("pe", C("matmul", PS.v(0, 128, ooff, [[1, 64]]), pb_.v(0, nkB, r * 128, [[1, 128]]), vcur.v(0, nkB, hk * 65, [[1, 64]]),
                                                                                                start=not hasA, stop=True), R=[pbn, vcn], W=[f"ps{obank}"])
                        T.op("pe", C("matmul", PS.v(0, 128, soff, [[1, 2]]), pb_.v(0, nkB, r * 128, [[1, 128]]), onesb.v(0, nkB, 0, [[1, 2]]),
                                                                              start=not hasA, stop=True), R=[pbn, "onesb"], W=[f"ps{sbank}"])
                recs = [T.rec(att_chain, hk) for hk in range(4)]
                T.play(recs[0:2])
                T.play(recs[2:4])
                stage(2.8)
                rn, rf = sm()
                ridx = int(rn[5:])
                for par in range(2):
                    sbank = 7 if par == 0 else 6
                    T.op("dve", C("tensor_tensor", out=small.v(0, 128, ridx * 32 + par * 4, [[8, 2], [1, 4]]), in0=PS.v(0, 128, psb(sbank, par * 8), [[16, 2], [2, 4]]),
                                  in1=esink.v(0, 128, par * 4, [[8, 2], [1, 4]]), op=ALU.add), R=[f"ps{sbank}", "esink"], W=[rn])
                T.op("dve", C("reciprocal", out=rf(w=16), in_=rf(w=16)), R=[rn], W=[rn])
                for par in range(2):
                    bank = 4 + par
                    T.op("dve", C("tensor_tensor", out=attn_f.v(0, 128, par * 256, [[512, 2], [64, 4], [1, 64]]), in0=PS.v(0, 128, psb(bank), [[256, 2], [64, 4], [1, 64]]),
                                  in1=small.v(0, 128, ridx * 32 + par * 4, [[8, 2], [1, 4], [0, 64]]), op=ALU.mult),
                         R=[f"ps{bank}", rn], W=["attn_f"])
                T.op("dve", C("tensor_tensor", out=full(attg, 1024), in0=full(attn_f, 1024), in1=full(sg_bf[ti], 1024), op=ALU.mult), R=["attn_f", f"sg_bf{ti}"], W=["attg"])
                stage(2.9)
                for kt in range(8):
                    T.op("pe", C("transpose", out=PSB.v(0, 128, 2 * psb(6) + kt * 128, [[1, 128]]), in_=attg.v(0, 128, kt * 128, [[1, 128]]), identity=full(identb, 128)),
                         R=["attg", "identb"], W=["ps6"])
                T.op("act", C("activation", out=attgT.v(0, 128, ti * 128, [[512, 8], [1, 128]]), in_=PSB.v(0, 128, 2 * psb(6), [[128, 8], [1, 128]]), func=AF.Copy),
                     R=["ps6"], W=["attgT"])

            stage(3)
            for c in range(2):
                s, nkt, nco = w_next("ga")
                for ct in range(4):
                    bank = ct % 2
                    formA(s, nkt, nco, ct, bank, hT, "hT")
                    T.op("act", C("activation", out=gsig.v(0, 128, ct * 512, [[1, NTOK]]), in_=PS.v(0, 128, psb(bank), [[1, NTOK]]), func=AF.Sigmoid),
                         R=[f"ps{bank}"], W=["gsig"])
                s, nkt, nco = w_next("ao")
                for ct in range(4):
                    bank = ct % 2
                    formA(s, nkt, nco, ct, bank, attgT, "attgT")
                    T.op("dve", C("tensor_tensor", out=t1m.v(0, 128, (c * 4 + ct) * 512, [[1, NTOK]]), in0=PS.v(0, 128, psb(bank), [[1, NTOK]]),
                                                                                in1=gsig.v(0, 128, ct * 512, [[1, NTOK]]), op=ALU.mult), R=[f"ps{bank}", "gsig"], W=["t1m"])

            stage(4)
            for c in range(4):
                s, nkt, nco = w_next("z")
                for ti, tl in enumerate(tiles):
                    bank = ti % 2
                    formB(s, nkt, nco, ti, bank)
                    T.op("act", C("activation", out=sz_bf[ti].v(0, 128, c * 512, [[1, 512]]), in_=PS.v(0, 128, psb(bank), [[1, 512]]), func=AF.Silu),
                         R=[f"ps{bank}"], W=[f"sz_bf{ti}"])
            stage(5)
            if is_s:
                for si, tl in enumerate(tiles):
                    T.dma("sp", full(sconv[si], 72), sconv_d[tl["s"]], W=[f"sconv{si}"])
            for c in range(6):
                s, nkt, nco = w_next("xbc")
                def conv_chain(c, ct, s, nkt, nco):
                    j = c * 4 + ct
                    bank = ct % 2
                    rb = j % 2
                    formA(s, nkt, nco, ct, bank, hT, "hT")
                    rw, rwn = raw_sb[rb], f"raw{rb}"
                    T.op("act", C("activation", out=rw.v(0, 128, 3, [[1, NTOK]]), in_=PS.v(0, 128, psb(bank), [[1, NTOK]]), func=AF.Copy), R=[f"ps{bank}"], W=[rwn])
                    if is_s:
                        T.op("pool", C("tensor_copy", out=rw.v(0, 128, 0, [[1, 3]]), in_=sconv[0].v(0, 128, j * 3, [[1, 3]])), R=["sconv0"], W=[rwn])
                        if NT > 1:
                            T.op("pool", C("tensor_copy", out=rw.v(0, 128, 3 + 125, [[1, 3]]), in_=sconv[1].v(0, 128, j * 3, [[1, 3]])), R=["sconv1"], W=[rwn])
                        for si in range(NT):
                            T.op("pool", C("tensor_copy", out=convo[si].v(0, 128, j * 3, [[1, 3]]), in_=rw.v(0, 128, 3 + si * 128 + 29, [[1, 3]])), R=[rwn], W=[f"convo{si}"])
                    else:
                        T.op("pool", C("tensor_copy", out=rw.v(0, 128, 0, [[1, 3]]), in_=tailb.v(0, 128, j * 3, [[1, 3]])), R=["tailb"], W=[rwn])
                        T.op("pool", C("tensor_copy", out=tailb.v(0, 128, j * 3, [[1, 3]]), in_=rw.v(0, 128, NTOK, [[1, 3]])), R=[rwn], W=["tailb"])
                    ac, acn = acc[rb], f"acc{rb}"
                    T.op("act", C("activation", out=ac.v(0, 128, 0, [[1, NTOK]]), in_=rw.v(0, 128, 3, [[1, NTOK]]), func=AF.Identity,
                                                                        scale=convw.v(0, 128, j * 4 + 3, [[1, 1]]), bias=convb.v(0, 128, j, [[1, 1]])), R=[rwn, "convw", "convb"], W=[acn])
                    for tap in (2, 1, 0):
                        T.op("dve", C("scalar_tensor_tensor", out=ac.v(0, 128, 0, [[1, NTOK]]), in0=rw.v(0, 128, tap, [[1, NTOK]]),
                                                                                               scalar=convw.v(0, 128, j * 4 + tap, [[1, 1]]), in1=ac.v(0, 128, 0, [[1, NTOK]]),
                                                                                               op0=ALU.mult, op1=ALU.add), R=[rwn, acn, "convw"], W=[acn])
                    if j < 16:
                        T.op("act", C("activation", out=xcfm[ct].v(0, 128, 0, [[1, NTOK]]), in_=ac.v(0, 128, 0, [[1, NTOK]]), func=AF.Silu), R=[acn], W=[f"xcfm{ct}"])
                    elif j < 20:
                        T.op("act", C("activation", out=BT.v(0, 128, (j - 16) * 512, [[1, NTOK]]), in_=ac.v(0, 128, 0, [[1, NTOK]]), func=AF.Silu), R=[acn], W=["BT"])
                    else:
                        T.op("act", C("activation", out=CT.v(0, 128, (j - 20) * 512, [[1, NTOK]]), in_=ac.v(0, 128, 0, [[1, NTOK]]), func=AF.Silu), R=[acn], W=["CT"])
                recs = [T.rec(conv_chain, c, ct, s, nkt, nco) for ct in range(4)]
                T.play(recs[0:2])
                T.play(recs[2:4])
                if c < 4:
                    for ti in range(NT):
                        bank = 2 + (ti // 2)
                        for ct in range(4):
                            T.op("pe", C("transpose", out=PSB.v(0, 128, 2 * psb(bank) + ((ti % 2) * 4 + ct) * 128, [[1, 128]]),
                                                                                     in_=xcfm[ct].v(0, 128, ti * 128, [[1, 128]]), identity=full(identb, 128)),
                                 R=[f"xcfm{ct}", "identb"], W=[f"ps{bank}"])
                    for ti in range(NT):
                        bank = 2 + (ti // 2)
                        T.op("dve", C("tensor_copy", out=xc[ti].v(0, 128, c * 512, [[1, 512]]), in_=PSB.v(0, 128, 2 * psb(bank) + (ti % 2) * 512, [[1, 512]])),
                             R=[f"ps{bank}"], W=[f"xc{ti}"])
                if c == 4:
                    for ti in range(NT):
                        bank = 2 + (ti // 2)
                        for g in range(4):
                            T.op("pe", C("transpose", out=PSB.v(0, 128, 2 * psb(bank) + ((ti % 2) * 4 + g) * 128, [[1, 128]]),
                                                                                   in_=BT.v(0, 128, g * 512 + ti * 128, [[1, 128]]), identity=full(identb, 128)),
                                 R=["BT", "identb"], W=[f"ps{bank}"])
                    for ti in range(NT):
                        bank = 2 + (ti // 2)
                        T.op("dve", C("tensor_copy", out=full(Btok[ti], 512), in_=PSB.v(0, 128, 2 * psb(bank) + (ti % 2) * 512, [[1, 512]])),
                             R=[f"ps{bank}"], W=[f"Btok{ti}"])
            if is_s:
                for si, tl in enumerate(tiles):
                    T.dma("sp", convs_o[tl["s"]], full(convo[si], 72), R=[f"convo{si}"], is_out=True)
            elif g0 == NG - 1:
                T.dma("sp", convp_o, full(tailb, 72), R=["tailb"], is_out=True)
            stage(6)
            s, nkt, nco = w_next("dt")
            for ti, tl in enumerate(tiles):
                bank = ti % 2
                formB(s, nkt, nco, ti, bank, n=32)
                T.op("dve", C("tensor_tensor", out=full(dtraw[ti], 32), in0=PS.v(0, 128, psb(bank), [[1, 32]]), in1=full(dtb, 32), op=ALU.add),
                     R=[f"ps{bank}", "dtb"], W=[f"dtraw{ti}"])

            stage(7)
            def tvv(p, k, w=32, p0=0, p1=128, off=0):
                return tv[p].v(p0, p1, k * 32 + off, [[1, w]])

            def ssd_A(ti, tl):
                p = ti % 2
                dtn, dan, nbn, ecn, ten, den = [f"tv{p}_{k}" for k in range(6)]
                T.op("act", C("activation", out=tvv(p, 0), in_=full(dtraw[ti], 32), func=AF.Exp), R=[f"dtraw{ti}"], W=[dtn])
                T.op("act", C("activation", out=tvv(p, 0), in_=tvv(p, 0), func=AF.Ln, bias=1.0), R=[dtn], W=[dtn])
                if is_s:
                    T.op("dve", C("memset", tvv(p, 0, p0=32, p1=64), 1e-30), W=[dtn])
                    T.op("dve", C("memset", tvv(p, 0, p0=64, p1=128), 1e-30), W=[dtn])
                T.op("dve", C("tensor_tensor", out=tvv(p, 1), in0=tvv(p, 0), in1=full(a_b, 32), op=ALU.mult), R=[dtn, "a_b"], W=[dan])
                T.op("pe", C("matmul", PS.v(0, 128, psb(7, 64), [[1, 32]]), full(Umat, 128), tvv(p, 1), start=True, stop=True), R=["Umat", dan], W=["ps7"])
                T.op("pe", C("matmul", PS.v(0, 128, psb(7, 96), [[1, 32]]), full(ones, 128), tvv(p, 1), start=True, stop=True), R=["ones", dan], W=["ps7"])
                T.op("dve", C("tensor_copy", out=dapad.v(0, 128, 0, [[1, 32]]), in_=tvv(p, 1)), R=[dan], W=["dapad"])
                T.op("pe", C("matmul", PS.v(0, 128, psb(7, 128), [[1, 128]]), full(dapad, 128), full(Umat, 128), start=True, stop=True), R=["Umat", "dapad"], W=["ps7"])
                T.op("act", C("activation", out=full(cumTs, 128, p1=32), in_=PS.v(0, 32, psb(7, 128), [[1, 128]]), func=AF.Copy), R=["ps7"], W=["cumTs"])
                T.dma("sp", cumscr[ti], full(cumTs, 128, p1=32), R=["cumTs"], W=[f"cumscr{ti}"])
                cum_ps = PS.v(0, 128, psb(7, 64), [[1, 32]])
                tot_ps = PS.v(0, 128, psb(7, 96), [[1, 32]])
                T.op("act", C("activation", out=tvv(p, 2), in_=tvv(p, 0), func=AF.Ln), R=[dtn], W=[nbn])
                T.op("dve", C("tensor_tensor", out=tvv(p, 2), in0=tvv(p, 2), in1=cum_ps, op=ALU.subtract), R=[nbn, "ps7"], W=[nbn])
                T.op("act", C("activation", out=tvv(p, 3), in_=cum_ps, func=AF.Exp), R=["ps7"], W=[ecn])
                T.op("act", C("activation", out=tvv(p, 4), in_=cum_ps, func=AF.Copy), R=["ps7"], W=[ten])
                T.op("dve", C("tensor_tensor", out=tvv(p, 4), in0=tot_ps, in1=tvv(p, 4), op=ALU.subtract), R=["ps7", ten], W=[ten])
                T.op("act", C("activation", out=tvv(p, 4), in_=tvv(p, 4), func=AF.Exp), R=[ten], W=[ten])
                T.op("dve", C("tensor_tensor", out=tvv(p, 4), in0=tvv(p, 4), in1=tvv(p, 0), op=ALU.mult), R=[ten, dtn], W=[ten])
                T.op("act", C("activation", out=tvv(p, 5), in_=tot_ps, func=AF.Exp), R=["ps7"], W=[den])
                for g in range(4):
                    T.op("pe", C("matmul", PS.v(0, 128, psb(6, g * 128), [[1, 128]]), BT.v(0, 128, g * 512 + ti * 128, [[1, 128]]), CT.v(0, 128, g * 512 + ti * 128, [[1, 128]]),
                                 start=True, stop=True), R=["BT", "CT"], W=["ps6"])
                for g in range(4):
                    cb = g % 2
                    T.dma("sp", full(cumB[cb], 1024), cumscr[ti, g * 8:(g + 1) * 8, :].rearrange("h i -> (h i)").partition_broadcast(128), R=[f"cumscr{ti}"], W=[f"cumB{cb}"])
                    T.op("pool", C("tensor_tensor", out=cumB[cb].v(0, 128, 0, [[128, 8], [1, 128]]), in0=cumB[cb].v(0, 128, 0, [[128, 8], [1, 128]]),
                                   in1=NEGm.v(0, 128, 0, [[0, 8], [1, 128]]), op=ALU.add), R=[f"cumB{cb}", "NEG"], W=[f"cumB{cb}"])
                    for hh in range(8):
                        h = g * 8 + hh
                        T.op("act", C("activation", out=Ldt[cb].v(0, 128, hh * 128, [[1, 128]]), in_=cumB[cb].v(0, 128, hh * 128, [[1, 128]]), func=AF.Exp,
                                      bias=tvv(p, 2, w=1, off=h)), R=[f"cumB{cb}", nbn], W=[f"Ldt{cb}_{hh}"])
                    T.op("dve", C("tensor_tensor", out=MTa[p].v(0, 128, g * 1024, [[128, 8], [1, 128]]), in0=Ldt[cb].v(0, 128, 0, [[128, 8], [1, 128]]),
                                  in1=PS.v(0, 128, psb(6, g * 128), [[0, 8], [1, 128]]), op=ALU.mult), R=[f"Ldt{cb}_{hh}" for hh in range(8)] + ["ps6"], W=[f"MT{p}_{g}"])

            def ssd_B(ti, tl):
                p = ti % 2
                dtn, dan, nbn, ecn, ten, den = [f"tv{p}_{k}" for k in range(6)]
                YB = [0, 1, 4, 5]
                if is_s:
                    T.dma("sp", full(state, 2048), sssm_d[tl["s"]], W=["state", "state0", "state1"])
                    T.op("act", C("activation", out=full(statebf, 2048), in_=full(state, 2048), func=AF.Copy), R=["state"], W=["statebf"])
                T.op("dve", C("tensor_tensor", out=xw.v(0, 128, 0, [[64, 32], [1, 64]]), in0=xc[ti].v(0, 128, 0, [[64, 32], [1, 64]]),
                              in1=tv[p].v(0, 128, 4 * 32, [[1, 32], [0, 64]]), op=ALU.mult), R=[f"xc{ti}", ten], W=["xw"])
                for g in range(4):
                    T.op("pe", C("matmul", PS.v(0, 128, psb(YB[g]), [[1, 512]]), CT.v(0, 128, g * 512 + ti * 128, [[1, 128]]), statebf.v(0, 128, g * 512, [[1, 512]]), start=True, stop=True),
                         R=["CT", "statebf"], W=[f"ps{YB[g]}"])
                for half in range(2):
                    b0 = YB[2 * half]
                    T.op("dve", C("tensor_tensor", out=yall.v(0, 128, half * 1024, [[64, 16], [1, 64]]), in0=PS.v(0, 128, psb(b0), [[64, 16], [1, 64]]),
                                  in1=tv[p].v(0, 128, 3 * 32 + half * 16, [[1, 16], [0, 64]]), op=ALU.mult), R=[f"ps{b0}", f"ps{b0 + 1}", ecn], W=[f"yall{half}"])
                for g in range(4):
                    T.op("pool", C("tensor_tensor", out=xD[0].v(0, 128, 0, [[64, 8], [1, 64]]), in0=xc[ti].v(0, 128, g * 512, [[64, 8], [1, 64]]),
                                   in1=dsk.v(0, 128, g * 8, [[1, 8], [0, 64]]), op=ALU.mult), R=[f"xc{ti}", "dsk"], W=["xD0"])
                    T.op("dve", C("tensor_tensor", out=yall.v(0, 128, g * 512, [[1, 512]]), in0=yall.v(0, 128, g * 512, [[1, 512]]), in1=full(xD[0], 512), op=ALU.add),
                         R=[f"yall{g // 2}", "xD0"], W=[f"yall{g // 2}"])
                for g in range(4):
                    for hh in range(8):
                        h = g * 8 + hh
                        T.op("pe", C("matmul", PS.v(0, 128, psb(YB[g], hh * 64), [[1, 64]]), MTa[p].v(0, 128, g * 1024 + hh * 128, [[1, 128]]), xc[ti].v(0, 128, h * 64, [[1, 64]]),
                                     start=True, stop=True), R=[f"MT{p}_{g}", f"xc{ti}"], W=[f"ps{YB[g]}"])
                for half in range(2):
                    b0 = YB[2 * half]
                    T.op("dve", C("tensor_tensor", out=yall.v(0, 128, half * 1024, [[1, 1024]]), in0=yall.v(0, 128, half * 1024, [[1, 1024]]), in1=PS.v(0, 128, psb(b0), [[1, 1024]]), op=ALU.add),
                         R=[f"yall{half}", f"ps{b0}", f"ps{b0 + 1}"], W=[f"yall{half}"])
                    T.op("dve", C("tensor_tensor", out=yall.v(0, 128, half * 1024, [[1, 1024]]), in0=yall.v(0, 128, half * 1024, [[1, 1024]]),
                                   in1=sz_bf[ti].v(0, 128, half * 1024, [[1, 1024]]), op=ALU.mult), R=[f"yall{half}", f"sz_bf{ti}"], W=[f"yall{half}"])
                ssn, ssf = sm()
                sidx = int(ssn[5:])
                for g in range(4):
                    T.op("act", C("activation", out=junk.v(0, 128, 0, [[1, 512]]), in_=yall.v(0, 128, g * 512, [[1, 512]]), func=AF.Square,
                                  accum_out=small.v(0, 128, sidx * 32 + g, [[1, 1]])), R=[f"yall{g // 2}"], W=["junk", ssn])
                rn, rf = rstd_from_ss(ssn, ssf(w=4), 512.0, width=4)
                ridx2 = int(rn[5:])
                T.op("dve", C("tensor_tensor", out=yzna.v(0, 128, 0, [[512, 4], [1, 512]]), in0=yall.v(0, 128, 0, [[512, 4], [1, 512]]),
                              in1=small.v(0, 128, ridx2 * 32, [[1, 4], [0, 512]]), op=ALU.mult), R=["yall0", "yall1", rn], W=["yzna"])
                for kt in range(16):
                    bank = 2 + kt // 8
                    T.op("pe", C("transpose", out=PSB.v(0, 128, 2 * psb(bank) + (kt % 8) * 128, [[1, 128]]), in_=yzna.v(0, 128, kt * 128, [[1, 128]]), identity=full(identb, 128)),
                         R=["yzna", "identb"], W=[f"ps{bank}"])
                for kt in range(16):
                    bank = 2 + kt // 8
                    T.op("act", C("activation", out=yzT.v(0, 128, kt * 512 + ti * 128, [[1, 128]]), in_=PSB.v(0, 128, 2 * psb(bank) + (kt % 8) * 128, [[1, 128]]), func=AF.Copy,
                                  scale=ssmw.v(0, 128, kt, [[1, 1]])), R=[f"ps{bank}", "ssmw"], W=[f"yzT{kt}"])
                for g in range(4):
                    T.op("pe", C("matmul", PS.v(0, 128, psb(YB[g]), [[1, 512]]), Btok[ti].v(0, 128, g * 128, [[1, 128]]), xw.v(0, 128, g * 512, [[1, 512]]), start=True, stop=True),
                         R=[f"Btok{ti}", "xw"], W=[f"ps{YB[g]}"])
                T.op("dve", C("tensor_tensor", out=state.v(0, 128, 0, [[64, 32], [1, 64]]), in0=state.v(0, 128, 0, [[64, 32], [1, 64]]),
                              in1=tv[p].v(0, 128, 5 * 32, [[1, 32], [0, 64]]), op=ALU.mult), R=["state0", "state1", "state", den], W=["state0", "state1"])
                for half in range(2):
                    b0 = YB[2 * half]
                    T.op("dve", C("tensor_tensor", out=state.v(0, 128, half * 1024, [[1, 1024]]), in0=state.v(0, 128, half * 1024, [[1, 1024]]), in1=PS.v(0, 128, psb(b0), [[1, 1024]]), op=ALU.add),
                         R=[f"state{half}", f"ps{b0}", f"ps{b0 + 1}"], W=[f"state{half}"])
                T.op("act", C("activation", out=full(statebf, 2048), in_=full(state, 2048), func=AF.Copy), R=["state0", "state1", "state"], W=["statebf"])
                if is_s:
                    T.dma("sp", ssms_o[tl["s"]], full(state, 2048), R=["state0", "state1"], is_out=True)
                elif tl["gi"] == NTP - 1:
                    T.dma("sp", ssmp_o, full(state, 2048), R=["state0", "state1"], is_out=True)

            recA = [T.rec(ssd_A, ti, tl) for ti, tl in enumerate(tiles)]
            recB = [T.rec(ssd_B, ti, tl) for ti, tl in enumerate(tiles)]
            T.play([recA[0]])
            for ti in range(NT):
                T.play([recB[ti]] + ([recA[ti + 1]] if ti + 1 < NT else []))

            stage(8)
            for c in range(2):
                s, nkt, nco = w_next("gs")
                for ct in range(4):
                    bank = ct % 2
                    formA(s, nkt, nco, ct, bank, hT, "hT")
                    T.op("act", C("activation", out=gsig.v(0, 128, ct * 512, [[1, NTOK]]), in_=PS.v(0, 128, psb(bank), [[1, NTOK]]), func=AF.Sigmoid),
                         R=[f"ps{bank}"], W=["gsig"])
                for c2 in range(2):
                    s, nkt, nco = w_next("so")
                    for ct in range(2):
                        bank = ct % 2
                        colt = c * 4 + c2 * 2 + ct
                        formA(s, nkt, nco, ct, bank, yzT, "yzT", per_kt=True)
                        T.op("dve", C("tensor_tensor", out=acc[0].v(0, 128, 0, [[1, NTOK]]), in0=PS.v(0, 128, psb(bank), [[1, NTOK]]),
                                                                                                 in1=gsig.v(0, 128, (c2 * 2 + ct) * 512, [[1, NTOK]]), op=ALU.mult), R=[f"ps{bank}", "gsig"], W=["acc0"])
                        T.op("pool", C("tensor_tensor", out=t1m.v(0, 128, colt * 512, [[1, NTOK]]), in0=acc[0].v(0, 128, 0, [[1, NTOK]]),
                                                                         in1=t1m.v(0, 128, colt * 512, [[1, NTOK]]), op=ALU.add), R=["acc0", "t1m"], W=["t1m"])
            stage(9)
            for c in range(2):
                s, nkt, nco = w_next("wo")
                for ti, tl in enumerate(tiles):
                    bank = ti % 2
                    for kt in range(8):
                        T.op("pe", C("matmul", PS.v(0, 128, psb(bank), [[1, 512]]), t1m.v(0, 128, kt * 512 + ti * 128, [[1, 128]]), wv(s, 8, 512, kt, 0, 512),
                                                                                  start=(kt == 0), stop=(kt == 7)), R=["t1m", f"wslot{s}"], W=[f"ps{bank}"])
                    T.op("act", C("activation", out=outf[ti].v(0, 128, c * 512, [[1, 512]]), in_=PS.v(0, 128, psb(bank), [[1, 512]]), func=AF.Copy),
                         R=[f"ps{bank}"], W=[f"outf{ti}"])

        onesb = sb("onesb", 2, BF16)
        dapad = sb("dapad", 128)
        T.op("dve", C("memset", full(dapad, 128), 0.0), W=["dapad"])
        T.op("dve", C("memset", full(onesb, 2), 1.0), W=["onesb"])

        try:
            stage(0)
            all_tiles = [[{"kind": "p", "gi": g0 * 4 + i} for i in range(4)] for g0 in range(NG)]
            if with_sample:
                all_tiles.append([{"kind": "s", "s": 0}, {"kind": "s", "s": 1}])
            T.play([T.rec(p0_emit, all_tiles[0])])
            for gidx, tl_ in enumerate(all_tiles):
                group(gidx, tl_)
                chains = [T.rec(post_emit, tl_)]
                if gidx + 1 < len(all_tiles):
                    chains.append(T.rec(p0_emit, all_tiles[gidx + 1]))
                T.play(chains)
        except _Stop:
            pass
        T.finish()

        block = es.enter_context(nc.Block())
        engmap = {"pe": block.tensor, "act": block.scalar, "dve": block.vector, "pool": block.gpsimd, "sp": block.sync}
        for en, deco in engmap.items():
            def body(e, en=en):
                for f in T.prog[en]:
                    f(e)
            deco(body)
    return nc, T


def host_tables(NTP):
    half = 8
    inv = (np.float32(500000.0) ** (-(np.arange(half, dtype=np.float32) * np.float32(2.0) / np.float32(16.0)))).astype(np.float32)
    cos2 = np.zeros((128, NTP + 1, 16), np.float32)
    sin2 = np.zeros((128, NTP + 1, 16), np.float32)
    for t in range(NTP + 1):
        pos = (np.arange(128, dtype=np.float32) + np.float32(128 * t)) if t < NTP else (np.float32(2048.0) + np.arange(128, dtype=np.float32))
        ang = (pos[:, None] * inv[None, :]).astype(np.float32)
        c, s_ = np.cos(ang).astype(np.float32), np.sin(ang).astype(np.float32)
        cos2[:, t, :8] = c
        cos2[:, t, 8:] = c
        sin2[:, t, :8] = s_
        sin2[:, t, 8:] = s_
    U = np.triu(np.ones((128, 128), np.float32))
    NEG = np.where(np.arange(128)[None, :] < np.arange(128)[:, None], np.float32(NEGV), np.float32(0.0)).astype(np.float32)
    ident = np.eye(128, dtype=np.float32)
    return cos2.reshape(128, -1), sin2.reshape(128, -1), U, NEG, ident


def make_in_maps(inp, NTP, n_cores):
    c2, s2, U, NEG, ident = host_tables(NTP)
    f = lambda a: np.ascontiguousarray(np.asarray(a, dtype=np.float32))
    shared = {
        "w_in": f(inp["w_in"][0]), "w_ao": f(inp["w_attn_o"][0]), "w_so": f(inp["w_ssm_o"][0]), "w_o": f(inp["w_out"][0]),
        "prew": f(inp["pre_norm_w"][0].reshape(8, 128).T), "ssmw": f(inp["ssm_norm_w"][0].reshape(16, 128).T),
        "postw": f(inp["post_norm_w"][0]),
        "convw": f(inp["conv_w"][0].reshape(4, 24, 128).transpose(2, 1, 0).reshape(128, 96)),
        "convb": f(inp["conv_b"][0].reshape(24, 128).T),
        "dtb": f(inp["dt_bias"][0]), "alog": f(inp["a_log"][0]), "dsk": f(inp["d_skip"][0]), "sinks": f(inp["sinks"][0]),
        "cos2": c2, "sin2": s2, "Umat": U, "NEGmat": NEG, "ident": ident,
    }
    maps = []
    for c in range(n_cores):
        m = dict(shared)
        m["x_p"] = f(inp["x_prompt"][c].reshape(-1, D)[:NTP * 128])
        sl = slice(2 * c, 2 * c + 2)
        m["x_s"] = f(inp["x_sample"][sl])
        ckk = np.asarray(inp["cache_k"][0, sl]).transpose(0, 2, 3, 1).reshape(2, 2, 2, 64, 128)
        m["ck"] = f(ckk.transpose(0, 2, 3, 1, 4).reshape(2, 128, 256))
        m["cv"] = f(inp["cache_v"][0, sl].reshape(2, 128, 256))
        m["sconv"] = f(inp["state_conv"][0, sl].reshape(2, 3, 24, 128).transpose(0, 3, 2, 1).reshape(2, 128, 72))
        m["sssm"] = f(inp["state_ssm"][0, sl].transpose(0, 3, 1, 2).reshape(2, 128, 2048))
        maps.append(m)
    return maps


_CACHE = {}


def run(inp, NTP, n_cores, trace=False):
    key = (NTP,)
    if key not in _CACHE:
        _CACHE[key] = build_program(NTP)[0]
    nc = _CACHE[key]
    maps = make_in_maps(inp, NTP, n_cores)
    res = run_bass_kernel_spmd(nc, maps, core_ids=list(range(n_cores)), trace=trace)
    R = res.results
    L = NTP * 128
    yp = np.stack([r["y_p"].reshape(L, D) for r in R])
    ys = np.concatenate([r["y_s"] for r in R], 0)
    kp = np.stack([r["kp_o"].reshape(128, 4, 64) for r in R])[None]
    vp = np.stack([r["vp_o"].reshape(128, 4, 64) for r in R])[None]
    cp = np.stack([r["convp_o"].reshape(128, 24, 3).transpose(2, 1, 0).reshape(3, 3072) for r in R])[None]
    sp = np.stack([r["ssmp_o"].reshape(128, 32, 64).transpose(1, 2, 0) for r in R])[None]
    ks = np.concatenate([r["ks_o"].reshape(2, 32, 4, 64) for r in R], 0)[None]
    vs = np.concatenate([r["vs_o"].reshape(2, 32, 4, 64) for r in R], 0)[None]
    cs = np.concatenate([r["convs_o"].reshape(2, 128, 24, 3).transpose(0, 3, 2, 1).reshape(2, 3, 3072) for r in R], 0)[None]
    ss = np.concatenate([r["ssms_o"].reshape(2, 128, 32, 64).transpose(0, 2, 3, 1) for r in R], 0)[None]
    outs = tuple(np.ascontiguousarray(a, dtype=np.float32) for a in (yp, ys, kp, vp, cp, sp, ks, vs, cs, ss))
    return outs, res


def kernel(**inputs):
    outs, _ = run(inputs, 64, 8)
    return outs
```
